# Optimizing a Trainium2 kernel written in Bass

```python
import math
import jax
import jax.numpy as jnp
from jax import lax
import numpy as np

D_MODEL = 2048
BATCH = 2
SEQ = 16384
DEPTH = 1
DEC_BATCH = 16
DEC_SEQ = 2048
PAST_LEN = 128

D_S5 = D_MODEL // 2
S5_GROUP = 16
S5_GROUPS = D_S5 // S5_GROUP
S5_STATE = 64
DT_MIN = 1e-3
DT_MAX = 1e-1
D_RWKV = D_MODEL // 2
RWKV_HEAD = 64
RWKV_HEADS = D_RWKV // RWKV_HEAD
LORA_RANK = 64
D_RW_IN = 3 * D_RWKV + 4 * LORA_RANK
N_IN = 2 * D_S5 + D_RW_IN + D_RWKV + 2 * D_MODEL
IN_SPLITS = (D_S5, 2 * D_S5, 2 * D_S5 + D_RW_IN, 2 * D_S5 + D_RW_IN + D_RWKV)
RW_SPLITS = (D_RWKV, 2 * D_RWKV, 3 * D_RWKV, 3 * D_RWKV + LORA_RANK, 3 * D_RWKV + 2 * LORA_RANK, 3 * D_RWKV + 3 * LORA_RANK)
RMS_EPS = 1e-6
GN_EPS = 64e-5
L2_EPS = 1e-12

kernel_name = 'bidir_s5_rwkv7_gated_hybrid'


def rms_norm(x, g):
    x32 = x.astype(jnp.float32)
    return x32 * lax.rsqrt(jnp.mean(jnp.square(x32), axis=-1, keepdims=True) + RMS_EPS) * g.astype(jnp.float32)


def centred_shift_mix(z, mu):
    zero = jnp.zeros_like(z[:, :1])
    prev = jnp.concatenate([zero, z[:, :-1]], axis=1)
    nxt = jnp.concatenate([z[:, 1:], zero], axis=1)
    return z + (0.5 * (prev + nxt) - z) * mu


def _ssm_combine(e_i, e_j):
    a_i, b_i = e_i
    a_j, b_j = e_j
    return a_j * a_i, a_j * b_i + b_j


def s5_direction(u, lam_re, lam_im, log_dt, b_re, b_im, c_re, c_im, reverse):
    f32 = jnp.float32
    lam = lax.complex(lam_re.astype(f32), lam_im.astype(f32))
    dt = jnp.exp(log_dt.astype(f32))[:, None]
    lam_bar = jnp.exp(lam * dt)
    b_bar = ((lam_bar - 1.0) / lam)[..., None] * lax.complex(b_re.astype(f32), b_im.astype(f32))
    c = lax.complex(c_re.astype(f32), c_im.astype(f32))

    def one_sequence(u_seq):
        bu = jnp.einsum('gph,lgh->lgp', b_bar, u_seq.astype(jnp.complex64))
        a = jnp.broadcast_to(lam_bar, bu.shape)
        _, states = lax.associative_scan(_ssm_combine, (a, bu), axis=0, reverse=reverse)
        return jnp.einsum('ghp,lgp->lgh', c, states).real

    return lax.map(one_sequence, u)


def s5_branch(u_flat, gate, lam_re, lam_im, log_dt, b_re, b_im, c_re, c_im, d, w_glu, w_out):
    bsz, seq, _ = u_flat.shape
    u = u_flat.astype(jnp.float32).reshape(bsz, seq, S5_GROUPS, S5_GROUP)
    y = d.astype(jnp.float32).reshape(S5_GROUPS, S5_GROUP) * u
    for dirn, rev in ((0, False), (1, True)):
        y = y + s5_direction(u, lam_re[dirn], lam_im[dirn], log_dt[dirn], b_re[dirn], b_im[dirn],
                             c_re[dirn], c_im[dirn], rev)
    z = jax.nn.gelu(y.reshape(bsz, seq, D_S5))
    z = z * jax.nn.sigmoid(z @ w_glu)
    z = z * jax.nn.silu(gate.astype(jnp.float32))
    return z @ w_out


def wkv7_scan(r, w, kk, a, k, v, reverse):
    bsz = r.shape[0]
    xs = tuple(jnp.moveaxis(t, 1, 0) for t in (r, w, kk, a, k, v))

    def step(s, inp):
        r_t, w_t, kk_t, a_t, k_t, v_t = inp
        sa = jnp.einsum('bhvk,bhk->bhv', s, -kk_t)
        s = (s * w_t[:, :, None, :] + sa[..., None] * (kk_t * a_t)[:, :, None, :]
             + v_t[..., None] * k_t[:, :, None, :])
        return s, jnp.einsum('bhvk,bhk->bhv', s, r_t)

    s0 = jnp.zeros((bsz, RWKV_HEADS, RWKV_HEAD, RWKV_HEAD), jnp.float32)
    _, ys = lax.scan(step, s0, xs, reverse=reverse)
    return jnp.moveaxis(ys, 0, 1)


def rwkv7_branch(rw_in, gate, mu, w0, w_up, a0, a_up, k_k, k_a, r_k, ln_w, ln_b, w_out):
    bsz, seq, _ = rw_in.shape
    z = centred_shift_mix(rw_in.astype(jnp.float32), mu)
    r, k, v, xw_f, xw_b, xa_f, xa_b = jnp.split(z, RW_SPLITS, axis=-1)

    def heads(t):
        return t.reshape(bsz, seq, RWKV_HEADS, RWKV_HEAD)

    kk = heads(k * k_k)
    kk = kk * lax.rsqrt(jnp.sum(kk * kk, axis=-1, keepdims=True) + L2_EPS)
    rh, vh = heads(r), heads(v)
    bonus = jnp.sum(rh * heads(k) * r_k, axis=-1, keepdims=True) * vh
    y = jnp.zeros_like(vh)
    for dirn, (xw, xa, rev) in enumerate(((xw_f, xa_f, False), (xw_b, xa_b, True))):
        logw = -jax.nn.softplus(-(w0[dirn] + jnp.tanh(xw) @ w_up[dirn])) - 0.5
        decay = jnp.exp(-jnp.exp(logw))
        a = jax.nn.sigmoid(a0[dirn] + xa @ a_up[dirn])
        kd = k * (1.0 + (a - 1.0) * k_a)
        y = y + wkv7_scan(rh, heads(decay), kk, heads(a), heads(kd), vh, rev)
    mean = jnp.mean(y, axis=-1, keepdims=True)
    var = jnp.mean(jnp.square(y - mean), axis=-1, keepdims=True)
    y = (y - mean) * lax.rsqrt(var + GN_EPS)
    y = y.reshape(bsz, seq, D_RWKV) * ln_w + ln_b + bonus.reshape(bsz, seq, D_RWKV)
    y = y * jax.nn.silu(gate.astype(jnp.float32))
    return y @ w_out


def hybrid_layer(x, pre_norm_g, post_norm_g, w_in, s5_lam_re, s5_lam_im, s5_log_dt, s5_b_re, s5_b_im,
                 s5_c_re, s5_c_im, s5_d, s5_w_glu, s5_w_out, rw_mu, rw_w0, rw_w_up, rw_a0, rw_a_up,
                 rw_k_k, rw_k_a, rw_r_k, rw_ln_w, rw_ln_b, rw_w_out, w_o):
    hn = rms_norm(x, pre_norm_g).astype(x.dtype)
    proj = hn @ w_in
    s5_u, s5_gate, rw_in, rw_gate, merge_logits = jnp.split(proj, IN_SPLITS, axis=-1)
    p_s5 = s5_branch(s5_u, s5_gate, s5_lam_re, s5_lam_im, s5_log_dt, s5_b_re, s5_b_im,
                     s5_c_re, s5_c_im, s5_d, s5_w_glu, s5_w_out)
    p_rw = rwkv7_branch(rw_in, rw_gate, rw_mu, rw_w0, rw_w_up, rw_a0, rw_a_up, rw_k_k, rw_k_a,
                        rw_r_k, rw_ln_w, rw_ln_b, rw_w_out)
    g_s5, g_rw = jnp.split(merge_logits.astype(jnp.float32), 2, axis=-1)
    h = jax.nn.sigmoid(g_s5) * p_s5 + jax.nn.sigmoid(g_rw) * p_rw
    out = h.astype(x.dtype) @ w_o
    return (x.astype(jnp.float32) + rms_norm(out, post_norm_g)).astype(x.dtype)


def encoder(x, params):
    for layer_idx in range(DEPTH):
        x = hybrid_layer(x, *[p[layer_idx] for p in params])
    return x


def setup_inputs(seed: int = 0) -> dict:
    key = jax.random.key(seed)
    ks = jax.random.split(key, 27)
    f32 = jnp.float32

    def nrm(k, shape, scale):
        return scale * jax.random.normal(k, shape, f32)

    s5_lam_shape = (DEPTH, 2, S5_GROUPS, S5_STATE)
    n_idx = jnp.arange(S5_STATE, dtype=f32)
    return {
        'x_prompt': nrm(ks[0], (BATCH, SEQ, D_MODEL), 1.0),
        'x_sample': nrm(ks[1], (DEC_BATCH, DEC_SEQ, D_MODEL), 1.0),
        'pre_norm_g': 1.0 + nrm(ks[2], (DEPTH, D_MODEL), 0.02),
        'post_norm_g': 1.0 + nrm(ks[3], (DEPTH, D_MODEL), 0.02),
        'w_in': nrm(ks[4], (DEPTH, D_MODEL, N_IN), D_MODEL ** -0.5),
        's5_lam_re': -0.5 + nrm(ks[5], s5_lam_shape, 0.01),
        's5_lam_im': math.pi * n_idx + nrm(ks[6], s5_lam_shape, 0.01),
        's5_log_dt': jax.random.uniform(ks[7], (DEPTH, 2, S5_GROUPS), f32, math.log(DT_MIN), math.log(DT_MAX)),
        's5_b_re': nrm(ks[8], (DEPTH, 2, S5_GROUPS, S5_STATE, S5_GROUP), (2 * S5_GROUP) ** -0.5),
        's5_b_im': nrm(ks[9], (DEPTH, 2, S5_GROUPS, S5_STATE, S5_GROUP), (2 * S5_GROUP) ** -0.5),
        's5_c_re': nrm(ks[10], (DEPTH, 2, S5_GROUPS, S5_GROUP, S5_STATE), S5_STATE ** -0.5),
        's5_c_im': nrm(ks[11], (DEPTH, 2, S5_GROUPS, S5_GROUP, S5_STATE), S5_STATE ** -0.5),
        's5_d': nrm(ks[12], (DEPTH, D_S5), 1.0),
        's5_w_glu': nrm(ks[13], (DEPTH, D_S5, D_S5), D_S5 ** -0.5),
        's5_w_out': nrm(ks[14], (DEPTH, D_S5, D_MODEL), D_S5 ** -0.5),
        'rw_mu': jax.random.uniform(ks[15], (DEPTH, D_RW_IN), f32),
        'rw_w0': jax.random.uniform(ks[16], (DEPTH, 2, D_RWKV), f32, -6.0, -1.0),
        'rw_w_up': nrm(ks[17], (DEPTH, 2, LORA_RANK, D_RWKV), 0.5 * LORA_RANK ** -0.5),
        'rw_a0': nrm(ks[18], (DEPTH, 2, D_RWKV), 0.5),
        'rw_a_up': nrm(ks[19], (DEPTH, 2, LORA_RANK, D_RWKV), 0.5 * LORA_RANK ** -0.5),
        'rw_k_k': 0.85 + nrm(ks[20], (DEPTH, D_RWKV), 0.05),
        'rw_k_a': 1.0 + nrm(ks[21], (DEPTH, D_RWKV), 0.05),
        'rw_r_k': nrm(ks[22], (DEPTH, RWKV_HEADS, RWKV_HEAD), 0.1),
        'rw_ln_w': 1.0 + nrm(ks[23], (DEPTH, D_RWKV), 0.02),
        'rw_ln_b': nrm(ks[24], (DEPTH, D_RWKV), 0.02),
        'rw_w_out': nrm(ks[25], (DEPTH, D_RWKV, D_MODEL), D_RWKV ** -0.5),
        'w_o': nrm(ks[26], (DEPTH, D_MODEL, D_MODEL), D_MODEL ** -0.5),
    }


def reference(x_prompt, x_sample, pre_norm_g, post_norm_g, w_in, s5_lam_re, s5_lam_im, s5_log_dt,
              s5_b_re, s5_b_im, s5_c_re, s5_c_im, s5_d, s5_w_glu, s5_w_out, rw_mu, rw_w0, rw_w_up,
              rw_a0, rw_a_up, rw_k_k, rw_k_a, rw_r_k, rw_ln_w, rw_ln_b, rw_w_out, w_o):
    params = (pre_norm_g, post_norm_g, w_in, s5_lam_re, s5_lam_im, s5_log_dt, s5_b_re, s5_b_im,
              s5_c_re, s5_c_im, s5_d, s5_w_glu, s5_w_out, rw_mu, rw_w0, rw_w_up, rw_a0, rw_a_up,
              rw_k_k, rw_k_a, rw_r_k, rw_ln_w, rw_ln_b, rw_w_out, w_o)
    y_prompt = encoder(x_prompt, params)
    y_sample = encoder(x_sample, params)
    return (y_prompt, y_sample)
```

```python
import math
import numpy as np
from contextlib import ExitStack
import concourse.bass as bass
import concourse.mybir as mybir
from concourse.bass_utils import run_bass_kernel_spmd

F32 = mybir.dt.float32
BF16 = mybir.dt.bfloat16
AF = mybir.ActivationFunctionType
ALU = mybir.AluOpType

D = 2048
TT = 512
CT = 64
NCK = TT // CT
NBK = TT // 8
N_IN = 10496
PI = math.pi
TWO_PI = 2.0 * math.pi
import os as _os
_STOP = int(_os.environ.get('S5STOP', '0'))


class Sched:
    def __init__(self, nc, same_engine_sync=True):
        self.nc = nc
        self.same = same_engine_sync
        self.eng = {"pe": nc.tensor, "act": nc.scalar, "dve": nc.vector, "pool": nc.gpsimd, "sp": nc.sync}
        self.streams = {}
        for s, e, inc in [("pe", "pe", 1), ("act", "act", 1), ("dve", "dve", 1), ("pool", "pool", 1)]:
            self.streams[s] = dict(eng=e, inc=inc, sem=None, cnt=0)
        self.dma_q = {"dsp": ("sp", 14), "dpool": ("pool", 10)}
        self.rr = {q: 0 for q in self.dma_q}
        for q, (e, n) in self.dma_q.items():
            for i in range(n):
                self.streams["%s%d" % (q, i)] = dict(eng=e, inc=16, sem=None, cnt=0)
        self.seen = {e: {s: 0 for s in self.streams} for e in self.eng}
        self.lastw = {}
        self.readers = {}
        self._ctx = []
        self.n_wait = 0
        self.n_ins = 0

    def open(self):
        for s, d in self.streams.items():
            cm = self.nc.semaphore("sem_" + s)
            d["sem"] = cm.__enter__()
            self._ctx.append(cm)

    def close(self):
        for cm in reversed(self._ctx):
            cm.__exit__(None, None, None)

    def _wait(self, e, s, c):
        if c <= 0 or self.seen[e][s] >= c:
            return
        d = self.streams[s]
        self.eng[e].wait_ge(d["sem"], c * d["inc"])
        self.seen[e][s] = c
        self.n_wait += 1

    def emit(self, stream, fn, reads=(), writes=(), acc=False, rg=0):
        if stream in self.dma_q:
            q = stream
            i = self.rr[q]
            self.rr[q] = (i + 1) % self.dma_q[q][1]
            stream = "%s%d" % (q, i)
            d = self.streams[stream]
            self._wait(d["eng"], stream, d["cnt"])
        d = self.streams[stream]
        e = d["eng"]
        deps = set()
        raw = set()
        for k in reads:
            if k in self.lastw:
                deps.add(self.lastw[k])
                raw.add(self.lastw[k])
        for k in writes:
            if k in self.lastw:
                deps.add(self.lastw[k])
            for r in self.readers.get(k, ()):
                deps.add(r)
        for (s2, c2) in deps:
            if s2 == stream and d["inc"] == 1:
                if stream == "pe" or not self.same or (s2, c2) not in raw:
                    continue
            self._wait(e, s2, c2)
        if stream == "pe":
            if rg != getattr(self, "last_rg", 0):
                self._wait(e, "pe", d["cnt"])
            self.last_rg = rg
        ins = fn(self.eng[e])
        d["cnt"] += 1
        ins.then_inc(d["sem"], d["inc"])
        me = (stream, d["cnt"])
        for k in writes:
            self.lastw[k] = me
            self.readers[k] = set()
        for k in reads:
            self.readers.setdefault(k, set()).add(me)
        self.n_ins += 1
        return ins

    def barrier(self):
        for e in self.eng:
            for s, d in self.streams.items():
                self._wait(e, s, d["cnt"])
        self.lastw = {}
        self.readers = {}

    def finish(self, e="sp"):
        for s, d in self.streams.items():
            self._wait(e, s, d["cnt"])


def _col(v):
    v = np.asarray(v, np.float32).reshape(-1, 128)
    return np.ascontiguousarray(v.T)


def _constants():
    c = {}
    c["identf"] = np.eye(128, dtype=np.float32)
    psw = np.zeros((128, 128), np.float32)
    for p in range(64):
        psw[p, 64 + p] = 1.0
        psw[64 + p, p] = 1.0
    c["pswap"] = psw
    bo = np.zeros((128, 128), np.float32)
    bo[:64, :64] = 1.0
    bo[64:, 64:] = 1.0
    c["bones"] = bo
    sp = np.zeros((128, 8, 240), np.float32)
    for g8 in range(8):
        for h in range(16):
            sp[g8 * 16 + h, g8, 112 + h] = 1.0
    c["selpad"] = sp
    c["seltpad"] = sp.copy()
    s_idx = np.arange(64)[:, None]
    t_idx = np.arange(64)[None, :]
    su = (s_idx < t_idx).astype(np.float32)
    ui = (s_idx <= t_idx).astype(np.float32)
    c["mask_g"] = np.concatenate([su, ui], axis=1)
    c["mask_lo"] = (s_idx > t_idx).astype(np.float32)
    rst = np.ones((128, TT), np.float32)
    rst[:, ::CT] = 0.0
    c["rst"] = rst
    c["ramp"] = np.tile(np.arange(1, 65, dtype=np.float32)[None, :], (128, 1))
    sg = np.ones((128, 1), np.float32)
    sg[:64] = -1.0
    c["sgna"] = sg
    return c


CONST_SHAPES = {"identf": [128, 128], "pswap": [128, 128], "bones": [128, 128], "selpad": [128, 8, 240],
                "seltpad": [128, 8, 240], "mask_g": [64, 128], "mask_lo": [64, 64], "rst": [128, TT],
                "ramp": [128, 64], "sgna": [128, 1]}


def _seq_bounds(g):
    if g < 32768:
        s = (g // 16384) * 16384
        return s, s + 16384
    s = 32768 + ((g - 32768) // 2048) * 2048
    return s, s + 2048


def make_plan(mode):
    if mode == "v1":
        runs = [[(0, 16384, 'f')], [(16384, 16384, 'f')]]
        groups = [[0, 1, 2], [3, 4, 5], [6, 7, 8], [9, 10, 11], [12, 13], [14, 15]]
        for gsel in groups:
            runs.append([(32768 + s * 2048, 2048, 'f') for s in gsel])
        return runs, 32
    if mode == "bal":
        runs = [[(0, 8192, 'f')], [(8192, 8192, 'r')], [(16384, 8192, 'f')], [(24576, 8192, 'r')]]
        for c in range(4):
            runs.append([(32768 + (4 * c + i) * 2048, 2048, 'f') for i in range(4)])
        return runs, 16
    raise ValueError(mode)


def host_layout(inputs, runs, NT, partners=None):
    f32 = np.float32
    xp = np.asarray(inputs["x_prompt"], f32).reshape(-1, D)
    xs = np.asarray(inputs["x_sample"], f32).reshape(-1, D)
    ntok = xp.shape[0] + xs.shape[0]
    x_pad = np.concatenate([xp, xs, np.zeros((1, D), f32)], axis=0)
    ZERO = ntok
    consts = _constants()
    g = lambda k: np.asarray(inputs[k], f32)[0]
    shared = {}
    shared["w_glu"] = g("s5_w_glu")
    shared["s5_w_out"] = g("s5_w_out")
    shared["rw_w_out"] = g("rw_w_out")
    shared["w_o"] = g("w_o")
    shared["pre_g"] = _col(g("pre_norm_g"))
    shared["post_g"] = np.ascontiguousarray(np.tile(g("post_norm_g")[None, :], (128, 1)))
    shared["k_k"] = _col(g("rw_k_k"))
    shared["k_a"] = _col(g("rw_k_a"))
    shared["r_k"] = _col(g("rw_r_k").reshape(-1))
    shared["ln_w"] = _col(g("rw_ln_w"))
    shared["ln_b"] = _col(g("rw_ln_b"))
    d = g("s5_d").reshape(64, 16)
    shared["d8"] = np.ascontiguousarray(np.tile(d.T, (8, 1)))
    shared.update(consts)

    def s5dir(dd, sfx):
        o = {}
        lre = g("s5_lam_re")[dd].T
        lim = g("s5_lam_im")[dd].T
        o["lamre" + sfx] = np.ascontiguousarray(np.concatenate([lre, lre], 0))
        o["lamim" + sfx] = np.ascontiguousarray(np.concatenate([lim, lim], 0))
        o["logdt" + sfx] = np.ascontiguousarray(np.tile(g("s5_log_dt")[dd][None, :], (128, 1)))
        bre = g("s5_b_re")[dd].transpose(1, 0, 2)
        bim = g("s5_b_im")[dd].transpose(1, 0, 2)
        o["bst" + sfx] = np.ascontiguousarray(np.concatenate([bre, bim], 0))
        o["bsw" + sfx] = np.ascontiguousarray(np.concatenate([bim, bre], 0))
        cre = g("s5_c_re")[dd].transpose(2, 0, 1)
        cim = g("s5_c_im")[dd].transpose(2, 0, 1)
        o["cst" + sfx] = np.ascontiguousarray(np.concatenate([cre, cim], 0))
        o["csw" + sfx] = np.ascontiguousarray(np.concatenate([cim, cre], 0))
        o["w0" + sfx] = _col(g("rw_w0")[dd])
        o["a0" + sfx] = _col(g("rw_a0")[dd])
        o["wup" + sfx] = np.ascontiguousarray(g("rw_w_up")[dd])
        o["aup" + sfx] = np.ascontiguousarray(g("rw_a_up")[dd])
        return o

    dir_cache = {}
    w_in_cache = {}
    w_in = g("w_in")
    mu = g("rw_mu")
    in_maps = []
    metas = []
    for cr in runs:
        dirA = 0 if cr[0][2] == 'f' else 1
        assert all((r[2] == 'f') == (dirA == 0) for r in cr)
        gidx = []
        runid = []
        for ri, (st, ln, dr) in enumerate(cr):
            ids = list(range(st, st + ln))
            if dr == 'r':
                ids = ids[::-1]
            gidx += ids
            runid += [ri] * ln
        assert len(gidx) % TT == 0 and len(gidx) <= NT * TT
        npad = NT * TT - len(gidx)
        gidx += [ZERO] * npad
        runid += [-1] * npad
        gidx = np.array(gidx, np.int64)
        runid = np.array(runid, np.int64)
        step = 1 if dirA == 0 else -1
        idxA = np.full((NT, TT + 2), ZERO, np.int64)
        for j in range(NT):
            ids = gidx[j * TT:(j + 1) * TT]
            idxA[j, 1:TT + 1] = ids
            if ids[0] != ZERO:
                lo, hi = _seq_bounds(int(ids[0]))
                b = int(ids[0]) - step
                if lo <= b < hi:
                    idxA[j, 0] = b
                a = int(ids[-1]) + step
                if lo <= a < hi:
                    idxA[j, TT + 1] = a
        xA = x_pad[idxA]
        xB = np.ascontiguousarray(xA[::-1, ::-1, :])
        tile_run = runid[::TT]
        cm = np.zeros((2 * NT,), f32)
        for j in range(1, NT):
            if tile_run[j] >= 0 and tile_run[j] == tile_run[j - 1]:
                cm[j] = 1.0
        for j in range(1, NT):
            a, b = tile_run[NT - 1 - j], tile_run[NT - j]
            if a >= 0 and a == b:
                cm[NT + j] = 1.0
        m = dict(shared)
        m["xA"] = np.ascontiguousarray(xA)
        m["xB"] = xB
        cm[NT] = 1.0
        m["cm_ab"] = cm
        xsel = np.zeros((128, 8), f32)
        ci = len(in_maps)
        if partners is not None and partners[ci] is not None:
            xsel[:, partners[ci]] = 1.0
        m["xsel"] = xsel
        if dirA not in dir_cache:
            oa = s5dir(dirA, "A")
            ob = s5dir(1 - dirA, "B")
            oa.update(ob)
            dir_cache[dirA] = oa
            if dirA == 0:
                w_in_cache[dirA] = (w_in, _col(mu))
            else:
                perm = np.arange(N_IN)
                base = 2048 + 3072
                perm[base:base + 256] = base + np.array(list(range(64, 128)) + list(range(0, 64)) +
                                                       list(range(192, 256)) + list(range(128, 192)))
                mperm = np.arange(3328)
                mperm[3072:] = perm[base:base + 256] - 2048
                w_in_cache[dirA] = (np.ascontiguousarray(w_in[:, perm]), _col(mu[mperm]))
        m.update(dir_cache[dirA])
        m["w_in"], m["mu"] = w_in_cache[dirA]
        in_maps.append(m)
        metas.append(dict(gidxB=gidx[::-1].copy()))
    zeroP = None
    for ci, m in enumerate(in_maps):
        pr = partners[ci] if partners is not None else None
        if pr is None:
            if zeroP is None:
                zeroP = np.zeros_like(m["xA"])
            m["xP"] = zeroP
            cmp_ = np.zeros((NT,), f32)
        else:
            m["xP"] = in_maps[pr]["xA"]
            cmp_ = in_maps[pr]["cm_ab"][:NT]
        m["cm"] = np.ascontiguousarray(np.tile(np.concatenate([cmp_, m["cm_ab"]])[None, :], (128, 1)))
    for m in in_maps:
        del m["cm_ab"]
    return in_maps, metas, ntok


def input_shapes(NT):
    sh = {"xA": [NT, TT + 2, D], "xB": [NT, TT + 2, D], "xP": [NT, TT + 2, D], "cm": [128, 3 * NT], "xsel": [128, 8],
          "w_in": [D, N_IN], "w_glu": [1024, 1024], "s5_w_out": [1024, D], "rw_w_out": [1024, D], "w_o": [D, D],
          "pre_g": [128, 16], "post_g": [128, D], "mu": [128, 26], "k_k": [128, 8], "k_a": [128, 8],
          "r_k": [128, 8], "ln_w": [128, 8], "ln_b": [128, 8], "d8": [128, 64]}
    sh.update(CONST_SHAPES)
    for X in "AB":
        for k in ("lamre", "lamim", "logdt"):
            sh[k + X] = [128, 64]
        for k in ("bst", "bsw", "cst", "csw"):
            sh[k + X] = [128, 64, 16]
        sh["w0" + X] = [128, 8]
        sh["a0" + X] = [128, 8]
        sh["wup" + X] = [64, 1024]
        sh["aup" + X] = [64, 1024]
    return sh


def bc(ap2, n):
    sh = list(ap2.shape)
    return ap2.unsqueeze(len(sh)).broadcast_to(sh + [n])


def bcmid(ap2, n):
    sh = list(ap2.shape)
    return ap2.unsqueeze(1).broadcast_to([sh[0], n] + sh[1:])


def gen_s5(nc, S, dr, X, add_d, S5W, S5T, r8_out):
    with ExitStack() as es:
        cnt = [0]

        def sb(shape, dt=F32):
            cnt[0] += 1
            return es.enter_context(nc.sbuf_tensor("g%s_%d" % (X, cnt[0]), shape, dt))

        def ps(shape, dt=F32):
            cnt[0] += 1
            return es.enter_context(nc.psum_tensor("gp%s_%d" % (X, cnt[0]), shape, dt))

        K = lambda n: "g" + X + n
        lamre = sb([128, 64]); lamim = sb([128, 64]); dtt = sb([128, 64]); aa = sb([128, 64]); phi = sb([128, 64])
        ER = sb([128, 9, 64]); EI = sb([128, 9, 64])
        t1 = sb([128, 64]); t2 = sb([128, 64]); t3 = sb([128, 64]); t4 = sb([128, 64])
        fre = sb([128, 64]); fs = sb([128, 64]); nfs = sb([128, 64])
        bst = sb([128, 64, 16]); bsw = sb([128, 64, 16]); cst = sb([128, 64, 16]); csw = sb([128, 64, 16])
        bbst = sb([128, 64, 16]); bbsw = sb([128, 64, 16]); u1 = sb([128, 64, 16]); u2 = sb([128, 64, 16])
        LB = sb([128, 64, 128]); CL = sb([128, 64, 256])
        bpad = [sb([128, 8, 128]), sb([128, 8, 128])]
        w5 = [sb([128, 4, 128], BF16), sb([128, 4, 128], BF16)]
        T0 = sb([128, 64, 64]); T1 = sb([128, 64, 64])
        identf = sb([128, 128]); pswap = sb([128, 128]); d8 = sb([128, 64]); ramp = sb([128, 64])
        sgna = sb([128, 1]); nsgna = sb([128, 1]); negpi = sb([128, 1]); th8 = sb([128, 64])
        psW = [ps([128, 512]), ps([128, 512])]
        ti64 = sb([128, 64], mybir.dt.int32); tiT = sb([128, 64, 64], mybir.dt.int32)

        def ld(t, name, key):
            S.emit("dsp", lambda e: e.dma_start(out=t[:], in_=dr[name]), writes=[K(key)])

        ld(lamre, "lamre" + X, "lamre"); ld(lamim, "lamim" + X, "lamim"); ld(dtt, "logdt" + X, "dt")
        ld(bst, "bst" + X, "bst"); ld(bsw, "bsw" + X, "bsw"); ld(cst, "cst" + X, "cst"); ld(csw, "csw" + X, "csw")
        ld(identf, "identf", "identf"); ld(pswap, "pswap", "pswap"); ld(d8, "d8", "d8"); ld(ramp, "ramp", "ramp")
        ld(sgna, "sgna", "sgna")
        V = lambda fn, r, w: S.emit("dve", fn, reads=[K(x) for x in r], writes=[K(x) for x in w])
        A = lambda fn, r, w: S.emit("act", fn, reads=[K(x) for x in r], writes=[K(x) for x in w])
        G = lambda fn, r, w: S.emit("pool", fn, reads=[K(x) for x in r], writes=[K(x) for x in w])

        def sin_reduced(dst, src, mul, add, tf, ti, dkey, rkeys):
            V(lambda e: e.tensor_scalar(out=dst[:], in0=src, scalar1=mul, scalar2=add, op0=ALU.mult, op1=ALU.add), rkeys, [dkey])
            V(lambda e: e.tensor_scalar(out=tf[:], in0=dst[:], scalar1=1.0 / TWO_PI, scalar2=None, op0=ALU.mult), [dkey], ["rtf"])
            V(lambda e: e.tensor_copy(out=ti[:], in_=tf[:]), ["rtf"], ["rti"])
            V(lambda e: e.tensor_copy(out=tf[:], in_=ti[:]), ["rti"], ["rtf"])
            V(lambda e: e.scalar_tensor_tensor(out=dst[:], in0=tf[:], scalar=-TWO_PI, in1=dst[:], op0=ALU.mult, op1=ALU.add), ["rtf", dkey], [dkey])
            V(lambda e: e.tensor_scalar(out=dst[:], in0=dst[:], scalar1=-PI, scalar2=PI, op0=ALU.max, op1=ALU.min), [dkey], [dkey])
            A(lambda e: e.activation(out=dst[:], in_=dst[:], func=AF.Sin), [dkey], [dkey])

        V(lambda e: e.memset(negpi[:], -PI), [], ["negpi"])
        V(lambda e: e.tensor_scalar(out=nsgna[:], in0=sgna[:], scalar1=-1.0, scalar2=None, op0=ALU.mult), ["sgna"], ["nsgna"])
        A(lambda e: e.activation(out=dtt[:], in_=dtt[:], func=AF.Exp), ["dt"], ["dt"])
        V(lambda e: e.tensor_tensor(out=aa[:], in0=lamre[:], in1=dtt[:], op=ALU.mult), ["lamre", "dt"], ["aa"])
        V(lambda e: e.tensor_tensor(out=phi[:], in0=lamim[:], in1=dtt[:], op=ALU.mult), ["lamim", "dt"], ["phi"])
        for j in range(9):
            A(lambda e, j=j: e.activation(out=t1[:], in_=aa[:], func=AF.Exp, scale=float(j)), ["aa"], ["t1"])
            sin_reduced(t3, phi[:], float(j), 0.0, t2, ti64, "t3", ["phi"])
            V(lambda e, j=j: e.tensor_tensor(out=EI[:, j, :], in0=t1[:], in1=t3[:], op=ALU.mult), ["t1", "t3"], ["EI%d" % j])
            sin_reduced(t4, phi[:], float(j), 0.5 * PI, t2, ti64, "t4", ["phi"])
            V(lambda e, j=j: e.tensor_tensor(out=ER[:, j, :], in0=t1[:], in1=t4[:], op=ALU.mult), ["t1", "t4"], ["ER%d" % j])
        if _STOP == 2:
            S.barrier(); return
        V(lambda e: e.tensor_tensor(out=t1[:], in0=lamre[:], in1=lamre[:], op=ALU.mult), ["lamre"], ["t1"])
        V(lambda e: e.tensor_tensor(out=t2[:], in0=lamim[:], in1=lamim[:], op=ALU.mult), ["lamim"], ["t2"])
        V(lambda e: e.tensor_tensor(out=t1[:], in0=t1[:], in1=t2[:], op=ALU.add), ["t1", "t2"], ["t1"])
        V(lambda e: e.reciprocal(out=t1[:], in_=t1[:]), ["t1"], ["t1"])
        V(lambda e: e.tensor_scalar(out=t2[:], in0=ER[:, 1, :], scalar1=-1.0, scalar2=None, op0=ALU.add), ["ER1"], ["t2"])
        V(lambda e: e.tensor_tensor(out=t3[:], in0=t2[:], in1=lamre[:], op=ALU.mult), ["t2", "lamre"], ["t3"])
        V(lambda e: e.tensor_tensor(out=t4[:], in0=EI[:, 1, :], in1=lamim[:], op=ALU.mult), ["EI1", "lamim"], ["t4"])
        V(lambda e: e.tensor_tensor(out=t3[:], in0=t3[:], in1=t4[:], op=ALU.add), ["t3", "t4"], ["t3"])
        V(lambda e: e.tensor_tensor(out=fre[:], in0=t3[:], in1=t1[:], op=ALU.mult), ["t3", "t1"], ["fre"])
        V(lambda e: e.tensor_tensor(out=t3[:], in0=EI[:, 1, :], in1=lamre[:], op=ALU.mult), ["EI1", "lamre"], ["t3"])
        V(lambda e: e.tensor_tensor(out=t4[:], in0=t2[:], in1=lamim[:], op=ALU.mult), ["t2", "lamim"], ["t4"])
        V(lambda e: e.tensor_tensor(out=t3[:], in0=t3[:], in1=t4[:], op=ALU.subtract), ["t3", "t4"], ["t3"])
        V(lambda e: e.tensor_tensor(out=t3[:], in0=t3[:], in1=t1[:], op=ALU.mult), ["t3", "t1"], ["t3"])
        V(lambda e: e.tensor_scalar(out=fs[:], in0=t3[:], scalar1=sgna[:, 0:1], scalar2=None, op0=ALU.mult), ["t3", "sgna"], ["fs"])
        V(lambda e: e.tensor_scalar(out=nfs[:], in0=fs[:], scalar1=-1.0, scalar2=None, op0=ALU.mult), ["fs"], ["nfs"])
        V(lambda e: e.tensor_tensor(out=u1[:], in0=bst[:], in1=bc(fre[:], 16), op=ALU.mult), ["bst", "fre"], ["u1"])
        V(lambda e: e.tensor_tensor(out=u2[:], in0=bsw[:], in1=bc(fs[:], 16), op=ALU.mult), ["bsw", "fs"], ["u2"])
        V(lambda e: e.tensor_tensor(out=bbst[:], in0=u1[:], in1=u2[:], op=ALU.add), ["u1", "u2"], ["bbst"])
        V(lambda e: e.tensor_tensor(out=u1[:], in0=bsw[:], in1=bc(fre[:], 16), op=ALU.mult), ["bsw", "fre"], ["u1"])
        V(lambda e: e.tensor_tensor(out=u2[:], in0=bst[:], in1=bc(nfs[:], 16), op=ALU.mult), ["bst", "nfs"], ["u2"])
        V(lambda e: e.tensor_tensor(out=bbsw[:], in0=u1[:], in1=u2[:], op=ALU.add), ["u1", "u2"], ["bbsw"])
        LB4 = LB[:].rearrange("p g (s h) -> p g s h", h=16)
        CL4 = CL[:].rearrange("p g (s h) -> p g s h", h=16)
        G(lambda e: e.memset(CL[:], 0.0), [], ["CL"])
        G(lambda e: e.memset(bpad[0][:], 0.0), [], ["bpad0"])
        G(lambda e: e.memset(bpad[1][:], 0.0), [], ["bpad1"])
        for j in range(8):
            V(lambda e, j=j: e.tensor_scalar(out=t1[:], in0=EI[:, j, :], scalar1=sgna[:, 0:1], scalar2=None, op0=ALU.mult), ["EI%d" % j, "sgna"], ["t1"])
            V(lambda e, j=j: e.tensor_tensor(out=u1[:], in0=bbst[:], in1=bc(ER[:, j, :], 16), op=ALU.mult), ["bbst", "ER%d" % j], ["u1"])
            V(lambda e: e.tensor_tensor(out=u2[:], in0=bbsw[:], in1=bc(t1[:], 16), op=ALU.mult), ["bbsw", "t1"], ["u2"])
            V(lambda e, j=j: e.tensor_tensor(out=LB4[:, :, 7 - j, :], in0=u1[:], in1=u2[:], op=ALU.add), ["u1", "u2"], ["LB"])
        for j in range(9):
            V(lambda e, j=j: e.tensor_scalar(out=t1[:], in0=ER[:, j, :], scalar1=nsgna[:, 0:1], scalar2=None, op0=ALU.mult), ["ER%d" % j, "nsgna"], ["t1"])
            V(lambda e, j=j: e.tensor_scalar(out=t2[:], in0=EI[:, j, :], scalar1=-1.0, scalar2=None, op0=ALU.mult), ["EI%d" % j], ["t2"])
            V(lambda e: e.tensor_tensor(out=u1[:], in0=cst[:], in1=bc(t1[:], 16), op=ALU.mult), ["cst", "t1"], ["u1"])
            V(lambda e: e.tensor_tensor(out=u2[:], in0=csw[:], in1=bc(t2[:], 16), op=ALU.mult), ["csw", "t2"], ["u2"])
            V(lambda e, j=j: e.tensor_tensor(out=CL4[:, :, 7 + j, :], in0=u1[:], in1=u2[:], op=ALU.add), ["u1", "u2"], ["CL"])
        if _STOP == 3:
            S.barrier(); return
        for g in range(64):
            k = g % 2
            bp = bpad[k]
            base = bp[:]
            dst = bass.AP(base.tensor, base.offset, [list(base.ap[0]), [128 + 16, 8], [1, 16]])
            G(lambda e, g=g, dst=dst: e.tensor_copy(out=dst, in_=bcmid(bbst[:, g, :], 8)), ["bbst"], ["bpad%d" % k])
            pw = psW[k]
            for s8 in range(8):
                S.emit("pe", lambda e, g=g, s8=s8, bp=bp, pw=pw: e.matmul(
                    pw[:, 0:128], lhsT=bp[:, s8, :], rhs=CL[:, g, (7 - s8) * 16:(7 - s8) * 16 + 128],
                    start=(s8 == 0), stop=(s8 == 7)),
                    reads=[K("bpad%d" % k), K("CL")], writes=[K("psW%d" % k)], acc=(s8 > 0))
            S.emit("pe", lambda e, g=g, pw=pw: e.matmul(pw[:, 128:256], lhsT=LB[:, g, :], rhs=identf[:], start=True, stop=True),
                   reads=[K("LB"), K("identf")], writes=[K("psW%d" % k)])
            S.emit("pe", lambda e, g=g, pw=pw: e.matmul(pw[:, 256:384], lhsT=LB[:, g, :], rhs=pswap[:], start=True, stop=True),
                   reads=[K("LB"), K("pswap")], writes=[K("psW%d" % k)], acc=True)
            w = w5[k]
            if add_d:
                V(lambda e, g=g, w=w, pw=pw: e.scalar_tensor_tensor(out=w[:, 0, :], in0=identf[:], scalar=d8[:, g:g + 1], in1=pw[:, 0:128],
                                                               op0=ALU.mult, op1=ALU.add),
                  ["identf", "d8", "psW%d" % k], ["w5_%d" % k])
            else:
                V(lambda e, w=w, pw=pw: e.tensor_copy(out=w[:, 0, :], in_=pw[:, 0:128]), ["psW%d" % k], ["w5_%d" % k])
            A(lambda e, w=w, pw=pw: e.copy(out=w[:, 1:3, :], in_=pw[:, 128:384].rearrange("p (a b) -> p a b", b=128)), ["psW%d" % k], ["w5_%d" % k])
            G(lambda e, g=g, w=w: e.tensor_copy(out=w[:, 3, :], in_=CL[:, g, 128:256]), ["CL"], ["w5_%d" % k])
            S.emit("dsp", lambda e, g=g, w=w: e.dma_start(out=S5W[g].rearrange("w r c -> r w c"), in_=w[:]),
                   reads=[K("w5_%d" % k)])
        if _STOP == 4:
            S.barrier(); return
        V(lambda e: e.tensor_scalar(out=th8[:], in0=phi[:], scalar1=8.0, scalar2=None, op0=ALU.mult), ["phi"], ["th8"])
        V(lambda e: e.tensor_scalar(out=t2[:], in0=th8[:], scalar1=1.0 / TWO_PI, scalar2=None, op0=ALU.mult), ["th8"], ["t2"])
        V(lambda e: e.tensor_copy(out=ti64[:], in_=t2[:]), ["t2"], ["rti"])
        V(lambda e: e.tensor_copy(out=t2[:], in_=ti64[:]), ["rti"], ["t2"])
        V(lambda e: e.scalar_tensor_tensor(out=th8[:], in0=t2[:], scalar=-TWO_PI, in1=th8[:], op0=ALU.mult, op1=ALU.add), ["t2", "th8"], ["th8"])
        if _STOP == 5:
            S.barrier(); return
        T0f = T0[:].rearrange("p g b -> p (g b)")
        T1f = T1[:].rearrange("p g b -> p (g b)")
        for ti, off in ((0, 0.5 * PI), (1, 0.0)):
            V(lambda e: e.tensor_tensor(out=T1[:], in0=bc(th8[:], 64), in1=bcmid(ramp[:], 64), op=ALU.mult), ["th8", "ramp"], ["T1"])
            if _STOP == 6:
                S.barrier(); return
            sin_reduced(T1, T1[:], 1.0, off, T0, tiT, "T1", ["T1"])
            if _STOP == 7:
                S.barrier(); return
            if ti == 1:
                V(lambda e: e.tensor_scalar(out=T1[:], in0=T1[:], scalar1=nsgna[:, 0:1], scalar2=None, op0=ALU.mult), ["T1", "nsgna"], ["T1"])
            for b8 in range(8):
                S.emit("dsp", lambda e, b8=b8, ti=ti: e.dma_start(out=S5T[b8, ti], in_=T1f[:, b8 * 512:(b8 + 1) * 512]), reads=[K("T1")])
        if _STOP == 8:
            S.barrier(); return
        A(lambda e: e.activation(out=t1[:], in_=aa[:], func=AF.Exp, scale=8.0), ["aa"], ["t1"])
        V(lambda e: e.tensor_copy(out=T0[:], in_=bc(t1[:], 64)), ["t1", "rtf"], ["T0", "rtf"])
        V(lambda e: e.tensor_scalar(out=T0[:, :, 0:1], in0=T0[:, :, 0:1], scalar1=0.0, scalar2=None, op0=ALU.mult), ["T0"], ["T0"])
        for b8 in range(8):
            S.emit("dsp", lambda e, b8=b8: e.dma_start(out=S5T[b8, 2], in_=T0f[:, b8 * 512:(b8 + 1) * 512]), reads=[K("T0")])
        V(lambda e: e.tensor_copy(out=r8_out, in_=t1[:]), ["t1"], ["r8" + X])
        S.barrier()


def build_program(NT, passes=("P", "A", "B"), n_cores=8):
    nc = bass.Bass("TRN2", target_bir_lowering=False)
    sh = input_shapes(NT)
    dr = {n: nc.dram_tensor(n, s, F32, kind="ExternalInput").ap() for n, s in sh.items()}
    yB = nc.dram_tensor("yB", [NT * TT, D], F32, kind="ExternalOutput").ap()
    Y1 = nc.dram_tensor("Y1s", [NT, 2048, TT], F32, kind="Internal").ap()
    S5W = {X: nc.dram_tensor("S5W" + X, [64, 4, 128, 128], BF16, kind="Internal").ap() for X in "AB"}
    S5T = {X: nc.dram_tensor("S5T" + X, [8, 3, 128, 512], F32, kind="Internal").ap() for X in "AB"}
    WBs = {"in": nc.dram_tensor("WB_in", [82, 128, 2048], BF16, kind="Internal").ap(),
           "glu": nc.dram_tensor("WB_glu", [8, 128, 1024], BF16, kind="Internal").ap(),
           "s5o": nc.dram_tensor("WB_s5o", [16, 128, 1024], BF16, kind="Internal").ap(),
           "rwo": nc.dram_tensor("WB_rwo", [16, 128, 1024], BF16, kind="Internal").ap(),
           "wo": nc.dram_tensor("WB_wo", [16, 128, 2048], BF16, kind="Internal").ap()}
    S = Sched(nc)
    S.open()
    with ExitStack() as es:
        cnt = [0]

        def sb(shape, dt=F32):
            cnt[0] += 1
            return es.enter_context(nc.sbuf_tensor("m_%d" % cnt[0], shape, dt))

        def ps(shape, dt=F32):
            cnt[0] += 1
            return es.enter_context(nc.psum_tensor("mp_%d" % cnt[0], shape, dt))

        r8 = {X: sb([128, 64]) for X in "AB"}
        gen_s5(nc, S, dr, "A", False, S5W["A"], S5T["A"], r8["A"][:])
        gen_s5(nc, S, dr, "B", True, S5W["B"], S5T["B"], r8["B"][:])

        V = lambda fn, r, w, **kw: S.emit("dve", fn, reads=r, writes=w, **kw)
        A = lambda fn, r, w, **kw: S.emit("act", fn, reads=r, writes=w, **kw)
        G = lambda fn, r, w, **kw: S.emit("pool", fn, reads=r, writes=w, **kw)
        P = lambda fn, r, w, **kw: S.emit("pe", fn, reads=r, writes=w, **kw)
        LD = lambda fn, r, w: S.emit("dsp", fn, reads=r, writes=w)
        LC = lambda fn, r, w: S.emit("dpool", fn, reads=r, writes=w)

        identf = sb([128, 128]); identb = sb([128, 128], BF16); bones = sb([128, 128])
        selb = sb([128, 8, 240], BF16); selt = sb([128, 8, 240]); mask_g = sb([64, 128]); mask_lo = sb([64, 64])
        rst = sb([128, TT]); cm = sb([128, 3 * NT])
        vec = {}
        for n in ("pre_g", "mu", "k_k", "k_a", "r_k", "ln_w", "ln_b", "w0A", "w0B", "a0A", "a0B"):
            vec[n] = sb(sh[n])
            LD(lambda e, n=n: e.dma_start(out=vec[n][:], in_=dr[n]), [], ["c_" + n])
        omm = sb([128, 26]); hmu = sb([128, 26]); omka = sb([128, 8])
        wup = sb([128, 1024]); aup = sb([128, 1024])
        for t, n in ((identf, "identf"), (bones, "bones"), (selt, "seltpad"), (mask_g, "mask_g"), (mask_lo, "mask_lo"),
                     (rst, "rst"), (cm, "cm")):
            LD(lambda e, t=t, n=n: e.dma_start(out=t[:], in_=dr[n]), [], ["c_" + n])
        LC(lambda e: e.dma_start(out=selb[:], in_=dr["selpad"]), [], ["c_selb"])
        LD(lambda e: e.dma_start(out=wup[0:64, :], in_=dr["wupA"]), [], ["c_wup"])
        LD(lambda e: e.dma_start(out=wup[64:128, :], in_=dr["wupB"]), [], ["c_wup"])
        LD(lambda e: e.dma_start(out=aup[0:64, :], in_=dr["aupA"]), [], ["c_aup"])
        LD(lambda e: e.dma_start(out=aup[64:128, :], in_=dr["aupB"]), [], ["c_aup"])
        V(lambda e: e.tensor_copy(out=identb[:], in_=identf[:]), ["c_identf"], ["c_identb"])
        V(lambda e: e.tensor_scalar(out=omm[:], in0=vec["mu"][:], scalar1=-1.0, scalar2=1.0, op0=ALU.mult, op1=ALU.add), ["c_mu"], ["c_omm"])
        V(lambda e: e.tensor_scalar(out=hmu[:], in0=vec["mu"][:], scalar1=0.5, scalar2=None, op0=ALU.mult), ["c_mu"], ["c_hmu"])
        V(lambda e: e.tensor_scalar(out=omka[:], in0=vec["k_a"][:], scalar1=-1.0, scalar2=1.0, op0=ALU.mult, op1=ALU.add), ["c_k_a"], ["c_omka"])

        NS = 18
        arena = sb([128, NS, 516])
        W = lambda i: arena[:, i, 0:512]
        WX = lambda i: arena[:, i, 0:514]
        wk = lambda *ids: ["w%d" % i for i in ids]
        hnT = sb([128, 16, TT + 2], BF16)
        xs = sb([128, D]); xn = sb([128, D], BF16); stat = sb([128, 8])
        wbuf = [sb([128, 16, 128], BF16), sb([128, 16, 128], BF16)]
        s5w = sb([128, 8, 4, 128], BF16); s5t = sb([128, 3, 512])
        ub = sb([128, 512], BF16); U8 = sb([128, 8, 64], BF16); Xp = sb([128, 8, 64], BF16)
        X1c = sb([128, 64]); X2c = sb([128, 64]); tmp8 = sb([128, 8]); tmp8b = sb([128, 8])
        LH = sb([128, 8, 128], BF16); RH = sb([128, 8, 128], BF16)
        Btok = sb([64, 8, 128], BF16); Ktok = sb([64, 8, 128], BF16); Vtok = sb([64, 8, 128], BF16)
        STb = sb([128, 8, 64], BF16); Vb = sb([128, 512], BF16)
        NbA = sb([64, 16, 64], BF16); Nt = sb([64, 16, 64], BF16); Nka = sb([64, 16, 64], BF16); Mbr = sb([64, 16, 64], BF16); Mkr = sb([64, 16, 64], BF16)
        Wm = sb([64, 16, 64], BF16); QT = sb([64, 128], BF16); QTf = sb([64, 128]); PT = sb([64, 128], BF16); YT = sb([64, 8, 128])
        ST = sb([128, 8, 64])
        zb = sb([128, 8, 512], BF16); zzb = sb([128, 8, 512], BF16); yrwb = sb([128, 8, 512], BF16); hb = sb([128, 16, 512], BF16)
        gsl = s5t[:, 0, :]
        pA = ps([128, 512]); pB = ps([128, 512]); pC = ps([128, 512]); pD = ps([128, 512])
        pP = [ps([128, 512]), ps([128, 512])]
        pT = ps([128, 1024], BF16)
        pH = ps([128, 16])
        PK = {id(pA): "pA", id(pB): "pB", id(pC): "pC", id(pD): "pD", id(pP[0]): "pP0", id(pP[1]): "pP1"}
        kk_ = lambda t: PK[id(t)]

        G(lambda e: e.memset(ST[:], 0.0), [], ["ST"])
        G(lambda e: e.memset(X1c[:], 0.0), [], ["X1c"])
        G(lambda e: e.memset(X2c[:], 0.0), [], ["X2c"])

        wstate = {"i": 0}
        w_in_v = dr["w_in"].rearrange("(kc p) m -> p kc m", p=128)

        def load_w(src_view, nk, width):
            i = wstate["i"]
            wstate["i"] = 1 - i
            wb = wbuf[i]
            dst = wb[:].rearrange("p a b -> p (a b)")[:, 0:nk * width].rearrange("p (a b) -> p a b", b=width)
            LC(lambda e: e.dma_start(out=dst, in_=src_view), [], ["wbuf%d" % i])
            return dst, "wbuf%d" % i

        def load_wb(name, blk, nk, width):
            i = wstate["i"]
            wstate["i"] = 1 - i
            wb = wbuf[i]
            flat = wb[:].rearrange("p a b -> p (a b)")[:, 0:nk * width]
            LD(lambda e: e.dma_start(out=flat, in_=WBs[name][blk]), ["WB_%s_%d" % (name, blk)], ["wbuf%d" % i])
            return flat.rearrange("p (a b) -> p a b", b=width), "wbuf%d" % i

        def precast(name, blk, src_view, nk, width):
            dst, key = load_w(src_view, nk, width)
            LD(lambda e: e.dma_start(out=WBs[name][blk], in_=dst.rearrange("p a b -> p (a b)")), [key], ["WB_%s_%d" % (name, blk)])

        pstate = {"i": 0}

        def proj(cb, halo):
            wv, wkey = load_wb("in", cb, 16, 128)
            i = pstate["i"]
            pstate["i"] = 1 - i
            pt = pP[i]
            for kc in range(16):
                P(lambda e, kc=kc: e.matmul(pt[:], lhsT=wv[:, kc, :], rhs=hnT[:, kc, 1:TT + 1], start=(kc == 0), stop=(kc == 15)),
                  [wkey, "hnT"], ["pP%d" % i], acc=(kc > 0))
            if halo:
                for kc in range(16):
                    P(lambda e, kc=kc: e.matmul(pH[:, 0:2], lhsT=wv[:, kc, :], rhs=hnT[:, kc, 0:TT + 2:TT + 1], start=(kc == 0), stop=(kc == 15)),
                      [wkey, "hnT"], ["pH"], acc=(kc > 0))
            return pt, "pP%d" % i

        def mixed(cb, mi, dst, dkey, t0, t1):
            pt, pk = proj(cb, True)
            A(lambda e: e.copy(out=arena[:, t0, 1:513], in_=pt[:]), [pk], wk(t0))
            V(lambda e: e.tensor_copy(out=arena[:, t0, 0:514:513], in_=pH[:, 0:2]), ["pH"], wk(t0))
            V(lambda e: e.tensor_tensor(out=W(t1), in0=arena[:, t0, 0:512], in1=arena[:, t0, 2:514], op=ALU.add), wk(t0), wk(t1))
            V(lambda e: e.tensor_scalar(out=arena[:, t0, 1:513], in0=arena[:, t0, 1:513], scalar1=omm[:, mi:mi + 1], scalar2=None, op0=ALU.mult),
              wk(t0) + ["c_omm"], wk(t0))
            V(lambda e: e.scalar_tensor_tensor(out=dst, in0=W(t1), scalar=hmu[:, mi:mi + 1], in1=arena[:, t0, 1:513], op0=ALU.mult, op1=ALU.add),
              wk(t0, t1) + ["c_hmu"], dkey)

        def stage_load(xsrc, j):
            for q in range(-1, 4):
                if q < 0:
                    npart = 2
                    LD(lambda e: e.dma_start(out=xs[0:2, :], in_=xsrc[j, 0:TT + 2:TT + 1, :]), [], ["xs"])
                else:
                    npart = 128
                    LD(lambda e, q=q: e.dma_start(out=xs[:, :], in_=xsrc[j, 1 + 128 * q:129 + 128 * q, :]), [], ["xs"])
                sl = slice(0, npart)
                V(lambda e: e.memset(stat[:], 0.0), [], ["stat"])
                A(lambda e, sl=sl: e.activation(out=xn[sl, :], in_=xs[sl, :], func=AF.Square, accum_out=stat[sl, 0:1]), ["xs", "stat"], ["xn", "stat"])
                V(lambda e, sl=sl: e.tensor_scalar(out=stat[sl, 1:2], in0=stat[sl, 0:1], scalar1=1.0 / D, scalar2=1e-6, op0=ALU.mult, op1=ALU.add), ["stat"], ["stat"])
                A(lambda e, sl=sl: e.activation(out=stat[sl, 1:2], in_=stat[sl, 1:2], func=AF.Sqrt), ["stat"], ["stat"])
                V(lambda e, sl=sl: e.reciprocal(out=stat[sl, 2:3], in_=stat[sl, 1:2]), ["stat"], ["stat"])
                A(lambda e, sl=sl: e.activation(out=xn[sl, :], in_=xs[sl, :], func=AF.Copy, scale=stat[sl, 2:3]), ["xs", "stat"], ["xn"])
                if q < 0:
                    for kc in range(16):
                        P(lambda e, kc=kc: e.transpose(out=pT[:, kc * 2:kc * 2 + 2], in_=xn[0:2, kc * 128:(kc + 1) * 128], identity=identb[0:2, 0:2]),
                          ["xn", "c_identb"], ["pT"], acc=(kc > 0))
                    V(lambda e: e.tensor_tensor(out=hnT[:, :, 0:TT + 2:TT + 1], in0=pT[:, 0:32].rearrange("p (a b) -> p a b", b=2),
                                                in1=bc(vec["pre_g"][:], 2), op=ALU.mult), ["pT", "c_pre_g"], ["hnT"])
                else:
                    for k4 in range(4):
                        for kk2 in range(4):
                            kc = k4 * 4 + kk2
                            P(lambda e, kc=kc, kk2=kk2: e.transpose(out=pT[:, kk2 * 128:(kk2 + 1) * 128], in_=xn[:, kc * 128:(kc + 1) * 128], identity=identb[:]),
                              ["xn", "c_identb"], ["pT"], acc=(kk2 > 0))
                        V(lambda e, k4=k4, q=q: e.tensor_tensor(out=hnT[:, k4 * 4:k4 * 4 + 4, 1 + 128 * q:129 + 128 * q],
                                                              in0=pT[:, 0:512].rearrange("p (a b) -> p a b", b=128),
                                                              in1=bc(vec["pre_g"][:, k4 * 4:k4 * 4 + 4], 128), op=ALU.mult), ["pT", "c_pre_g"], ["hnT"])

        def stage_s5(X, j, cb, last_pass, store=True):
            pt, pk = proj(cb, False)
            A(lambda e: e.copy(out=ub[:], in_=pt[:]), [pk], ["ub"])
            LD(lambda e: e.dma_start(out=s5w[:], in_=S5W[X][cb * 8:(cb + 1) * 8].rearrange("g w r c -> r g w c")), [], ["s5w"])
            LD(lambda e: e.dma_start(out=s5t[:], in_=S5T[X][cb].rearrange("t p f -> p t f")), [], ["s5t"])
            for g8 in range(8):
                for s8 in range(8):
                    P(lambda e, g8=g8, s8=s8: e.matmul(pA[:, g8 * 64:(g8 + 1) * 64], lhsT=selb[:, g8, 112 - 16 * s8:240 - 16 * s8],
                                                        rhs=ub[:, s8:512:8], start=(s8 == 0), stop=(s8 == 7)),
                      ["c_selb", "ub"], ["pA"], acc=not (g8 == 0 and s8 == 0))
            V(lambda e: e.tensor_copy(out=U8[:], in_=pA[:].rearrange("p (a b) -> p a b", b=64)), ["pA"], ["U8"])
            for g8 in range(8):
                P(lambda e, g8=g8: e.matmul(pC[:, g8 * 64:(g8 + 1) * 64], lhsT=s5w[:, g8, 1, :], rhs=U8[:, g8, :], start=True, stop=True),
                  ["s5w", "U8"], ["pC"], acc=(g8 > 0))
            for g8 in range(8):
                P(lambda e, g8=g8: e.matmul(pD[:, g8 * 64:(g8 + 1) * 64], lhsT=s5w[:, g8, 2, :], rhs=U8[:, g8, :], start=True, stop=True),
                  ["s5w", "U8"], ["pD"], acc=(g8 > 0))
            Ct, S1t, Rt = s5t[:, 0, :], s5t[:, 1, :], s5t[:, 2, :]
            V(lambda e: e.tensor_tensor(out=W(0), in0=pC[:], in1=Ct, op=ALU.mult), ["pC", "s5t"], wk(0))
            V(lambda e: e.tensor_tensor(out=W(1), in0=pD[:], in1=S1t, op=ALU.mult), ["pD", "s5t"], wk(1))
            G(lambda e: e.tensor_tensor(out=W(2), in0=W(0), in1=W(1), op=ALU.add), wk(0, 1), wk(2))
            V(lambda e: e.tensor_tensor(out=W(0), in0=pD[:], in1=Ct, op=ALU.mult), ["pD", "s5t"], wk(0))
            V(lambda e: e.tensor_tensor(out=W(1), in0=pC[:], in1=S1t, op=ALU.mult), ["pC", "s5t"], wk(1))
            G(lambda e: e.tensor_tensor(out=W(3), in0=W(0), in1=W(1), op=ALU.subtract), wk(0, 1), wk(3))
            gs = slice(cb * 8, cb * 8 + 8)
            V(lambda e: e.tensor_tensor(out=tmp8[:], in0=r8[X][:, gs], in1=X1c[:, gs], op=ALU.mult), ["X1c"], ["tmp8"])
            V(lambda e: e.tensor_tensor(out=arena[:, 2, 0:512:64], in0=arena[:, 2, 0:512:64], in1=tmp8[:], op=ALU.add), wk(2) + ["tmp8"], wk(2))
            V(lambda e: e.tensor_tensor(out=tmp8[:], in0=r8[X][:, gs], in1=X2c[:, gs], op=ALU.mult), ["X2c"], ["tmp8"])
            V(lambda e: e.tensor_tensor(out=arena[:, 3, 0:512:64], in0=arena[:, 3, 0:512:64], in1=tmp8[:], op=ALU.add), wk(3) + ["tmp8"], wk(3))
            V(lambda e: e.tensor_tensor_scan(out=W(4), data0=Rt, data1=W(2), initial=0.0, op0=ALU.mult, op1=ALU.add), wk(2) + ["s5t"], wk(4))
            V(lambda e: e.tensor_tensor_scan(out=W(5), data0=Rt, data1=W(3), initial=0.0, op0=ALU.mult, op1=ALU.add), wk(3) + ["s5t"], wk(5))
            V(lambda e: e.tensor_tensor(out=W(0), in0=W(4), in1=Ct, op=ALU.mult), wk(4) + ["s5t"], wk(0))
            V(lambda e: e.tensor_tensor(out=W(1), in0=W(5), in1=S1t, op=ALU.mult), wk(5) + ["s5t"], wk(1))
            G(lambda e: e.tensor_tensor(out=W(2), in0=W(0), in1=W(1), op=ALU.subtract), wk(0, 1), wk(2))
            x1v = W(2).rearrange("p (a b) -> p a b", b=64)
            G(lambda e: e.tensor_copy(out=Xp[:, :, 1:64], in_=x1v[:, :, 0:63]), wk(2), ["Xp"])
            G(lambda e: e.tensor_copy(out=Xp[:, :, 0:1], in_=X1c[:, gs].unsqueeze(2)), ["X1c"], ["Xp"])
            V(lambda e: e.tensor_copy(out=X1c[:, gs], in_=arena[:, 2, 63:512:64]), wk(2) + ["Xp"], ["X1c"])
            V(lambda e: e.tensor_tensor(out=tmp8[:], in0=arena[:, 5, 63:512:64], in1=s5t[:, 0, 63:512:64], op=ALU.mult), wk(5) + ["s5t"], ["tmp8"])
            V(lambda e: e.tensor_tensor(out=tmp8b[:], in0=arena[:, 4, 63:512:64], in1=s5t[:, 1, 63:512:64], op=ALU.mult), wk(4) + ["s5t"], ["tmp8b"])
            V(lambda e: e.tensor_tensor(out=X2c[:, gs], in0=tmp8[:], in1=tmp8b[:], op=ALU.add), ["tmp8", "tmp8b"], ["X2c"])
            if not last_pass and not store:
                return
            for g8 in range(8):
                P(lambda e, g8=g8: e.matmul(pB[:, g8 * 64:(g8 + 1) * 64], lhsT=s5w[:, g8, 0, :], rhs=U8[:, g8, :], start=True, stop=False),
                  ["s5w", "U8"], ["pB"], acc=(g8 > 0))
                P(lambda e, g8=g8: e.matmul(pB[:, g8 * 64:(g8 + 1) * 64], lhsT=s5w[:, g8, 3, :], rhs=Xp[:, g8, :], start=False, stop=True),
                  ["s5w", "Xp"], ["pB"], acc=True)
            A(lambda e: e.copy(out=W(3), in_=pB[:]), ["pB"], wk(3))
            y8 = W(3).rearrange("p (a b) -> p a b", b=64)
            for t8 in range(8):
                for g8 in range(8):
                    P(lambda e, g8=g8, t8=t8: e.matmul(pA[:, t8 * 64:(t8 + 1) * 64], lhsT=selt[:, t8, 112 - 16 * g8:240 - 16 * g8],
                                                        rhs=y8[:, g8, :], start=(g8 == 0), stop=(g8 == 7)),
                      ["c_seltpad"] + wk(3), ["pA"], acc=not (g8 == 0 and t8 == 0))
            V(lambda e: e.tensor_copy(out=W(6).rearrange("p (b t) -> p t b", t=8), in_=pA[:].rearrange("p (t b) -> p t b", b=64)), ["pA"], wk(6))
            if not last_pass:
                if store:
                    LD(lambda e: e.dma_start(out=Y1[j, cb * 128:(cb + 1) * 128, :], in_=W(6)), wk(6), ["Y1_%d" % j])
            else:
                jm = NT - 1 - j
                LD(lambda e: e.dma_start(out=W(7), in_=Y1[jm, cb * 128:(cb + 1) * 128, :]), ["Y1_%d" % jm], wk(7))
                rev = arena[:, 7, 511::-1] if False else bass.AP(arena[:].tensor, arena[:, 7, 511:512].offset, [list(arena[:].ap[0]), [-1, 512]])
                V(lambda e: e.tensor_tensor(out=W(0), in0=W(6), in1=rev, op=ALU.add), wk(6, 7), wk(0))
                A(lambda e: e.activation(out=W(1), in_=W(0), func=AF.Square), wk(0), wk(1))
                V(lambda e: e.tensor_scalar(out=W(1), in0=W(1), scalar1=0.044715, scalar2=1.0, op0=ALU.mult, op1=ALU.add), wk(1), wk(1))
                V(lambda e: e.tensor_tensor(out=W(1), in0=W(1), in1=W(0), op=ALU.mult), wk(0, 1), wk(1))
                A(lambda e: e.activation(out=W(1), in_=W(1), func=AF.Sigmoid, scale=1.5957691216057308), wk(1), wk(1))
                V(lambda e: e.tensor_tensor(out=zb[:, cb, :], in0=W(0), in1=W(1), op=ALU.mult), wk(0, 1), ["zb"])

        LORA0, LORA1 = 16, 17

        def stage_lora(X):
            mixed(40, 24, W(LORA0), wk(LORA0), 0, 1)
            A(lambda e: e.activation(out=W(LORA0), in_=W(LORA0), func=AF.Tanh), wk(LORA0), wk(LORA0))
            mixed(41, 25, W(LORA1), wk(LORA1), 0, 1)

        def stage_rwkv(X, j, hp, last_pass, store=True):
            lo = 0 if X == "A" else 64
            R_, K_, V_, LD_, AA_, KK_, BON_, BE_, KD_, CUM_, EN_, EP_, EV_ = 3, 4, 5, 6, 7, 8, 9, 10, 11, 12, 13, 14, 15
            mixed(16 + hp, hp, W(R_), wk(R_), 0, 1)
            mixed(24 + hp, 8 + hp, W(K_), wk(K_), 0, 1)
            mixed(32 + hp, 16 + hp, W(V_), wk(V_), 0, 1)
            if X == _os.environ.get('MPASS', 'A') and int(_os.environ.get('RSTOP', '0')) == 1:
                return
            hs = slice(hp * 128, (hp + 1) * 128)
            col = lambda t: t[:, hp:hp + 1]
            P(lambda e: e.matmul(pA[:], lhsT=wup[lo:lo + 64, hs], rhs=arena[lo:lo + 64, LORA0, 0:512], start=True, stop=True), ["c_wup"] + wk(LORA0), ["pA"], rg=lo)
            A(lambda e: e.activation(out=W(LD_), in_=pA[:], func=AF.Sigmoid, bias=col(vec["w0" + X])), ["pA", "c_w0" + X], wk(LD_))
            V(lambda e: e.tensor_scalar(out=W(LD_), in0=W(LD_), scalar1=-math.exp(-0.5), scalar2=None, op0=ALU.mult), wk(LD_), wk(LD_))
            P(lambda e: e.matmul(pB[:], lhsT=aup[lo:lo + 64, hs], rhs=arena[lo:lo + 64, LORA1, 0:512], start=True, stop=True), ["c_aup"] + wk(LORA1), ["pB"], rg=lo)
            A(lambda e: e.activation(out=W(AA_), in_=pB[:], func=AF.Sigmoid, bias=col(vec["a0" + X])), ["pB", "c_a0" + X], wk(AA_))
            V(lambda e: e.tensor_scalar(out=W(KK_), in0=W(K_), scalar1=col(vec["k_k"]), scalar2=None, op0=ALU.mult), wk(K_) + ["c_k_k"], wk(KK_))
            A(lambda e: e.activation(out=W(0), in_=W(KK_), func=AF.Square), wk(KK_), wk(0))
            P(lambda e: e.matmul(pC[:], lhsT=bones[:], rhs=W(0), start=True, stop=True), ["c_bones"] + wk(0), ["pC"])
            V(lambda e: e.tensor_scalar(out=W(1), in0=pC[:], scalar1=1e-12, scalar2=None, op0=ALU.add), ["pC"], wk(1))
            A(lambda e: e.activation(out=W(1), in_=W(1), func=AF.Sqrt), wk(1), wk(1))
            V(lambda e: e.reciprocal(out=W(1), in_=W(1)), wk(1), wk(1))
            V(lambda e: e.tensor_tensor(out=W(KK_), in0=W(KK_), in1=W(1), op=ALU.mult), wk(KK_, 1), wk(KK_))
            if last_pass:
                V(lambda e: e.scalar_tensor_tensor(out=W(0), in0=W(R_), scalar=col(vec["r_k"]), in1=W(K_), op0=ALU.mult, op1=ALU.mult), wk(R_, K_) + ["c_r_k"], wk(0))
                P(lambda e: e.matmul(pD[:], lhsT=bones[:], rhs=W(0), start=True, stop=True), ["c_bones"] + wk(0), ["pD"])
                V(lambda e: e.tensor_tensor(out=W(BON_), in0=pD[:], in1=W(V_), op=ALU.mult), ["pD"] + wk(V_), wk(BON_))
            V(lambda e: e.tensor_tensor(out=W(BE_), in0=W(KK_), in1=W(AA_), op=ALU.mult), wk(KK_, AA_), wk(BE_))
            V(lambda e: e.tensor_scalar(out=W(1), in0=W(AA_), scalar1=col(vec["k_a"]), scalar2=col(omka), op0=ALU.mult, op1=ALU.add), wk(AA_) + ["c_k_a", "c_omka"], wk(1))
            V(lambda e: e.tensor_tensor(out=W(KD_), in0=W(K_), in1=W(1), op=ALU.mult), wk(K_, 1), wk(KD_))
            V(lambda e: e.tensor_tensor_scan(out=W(CUM_), data0=rst[:], data1=W(LD_), initial=0.0, op0=ALU.mult, op1=ALU.add), wk(LD_) + ["c_rst"], wk(CUM_))
            A(lambda e: e.activation(out=W(EN_), in_=W(CUM_), func=AF.Exp, scale=-1.0), wk(CUM_), wk(EN_))
            A(lambda e: e.activation(out=W(EP_), in_=W(CUM_), func=AF.Exp), wk(CUM_), wk(EP_))
            V(lambda e: e.tensor_tensor(out=W(1), in0=W(CUM_), in1=W(LD_), op=ALU.subtract), wk(CUM_, LD_), wk(1))
            A(lambda e: e.activation(out=W(EV_), in_=W(1), func=AF.Exp), wk(1), wk(EV_))
            if X == _os.environ.get('MPASS', 'A') and int(_os.environ.get('RSTOP', '0')) == 2:
                return
            c3 = lambda i: W(i).rearrange("p (c t) -> p c t", t=64)
            V(lambda e: e.tensor_tensor(out=LH[:, :, 0:64], in0=c3(BE_), in1=c3(EN_), op=ALU.mult), wk(BE_, EN_), ["LH"])
            G(lambda e: e.tensor_tensor(out=LH[:, :, 64:128], in0=c3(KD_), in1=c3(EN_), op=ALU.mult), wk(KD_, EN_), ["LH"])
            V(lambda e: e.scalar_tensor_tensor(out=RH[:, :, 0:64], in0=c3(KK_), scalar=-1.0, in1=c3(EV_), op0=ALU.mult, op1=ALU.mult), wk(KK_, EV_), ["RH"])
            G(lambda e: e.tensor_tensor(out=RH[:, :, 64:128], in0=c3(R_), in1=c3(EP_), op=ALU.mult), wk(R_, EP_), ["RH"])
            if X == _os.environ.get('MPASS', 'A') and int(_os.environ.get('RSTOP', '0')) == 3:
                return
            V(lambda e: e.tensor_copy(out=Vb[:], in_=W(V_)), wk(V_), ["Vb"])
            for (src_fn, dstt, dk, rk) in ((lambda c: LH[:, c, 0:64], Btok, "Btok", ["LH"]), (lambda c: LH[:, c, 64:128], Ktok, "Ktok", ["LH"]),
                                           (lambda c: Vb[:, c * 64:(c + 1) * 64], Vtok, "Vtok", ["Vb"])):
                for c in range(8):
                    P(lambda e, c=c, src_fn=src_fn: e.transpose(out=pT[0:64, c * 128:(c + 1) * 128], in_=src_fn(c), identity=identb[:]),
                      rk + ["c_identb"], ["pT"], acc=(c > 0))
                V(lambda e, dstt=dstt: e.tensor_copy(out=dstt[:], in_=pT[0:64, :].rearrange("p (a b) -> p a b", b=128)), ["pT"], [dk])
            if X == _os.environ.get('MPASS', 'A') and int(_os.environ.get('RSTOP', '0')) == 4:
                return
            for h in range(2):
                ph = slice(64 * h, 64 * h + 64)
                for (lcol, pts, outs) in ((slice(0, 64), (pA, pB), (NbA, Mbr)), (slice(64, 128), (pC, pD), (Nka, Mkr))):
                    for half in range(2):
                        pt_ = pts[half]
                        for c4 in range(4):
                            c = half * 4 + c4
                            P(lambda e, c=c, c4=c4, pt_=pt_, lcol=lcol, ph=ph: e.matmul(pt_[0:64, c4 * 128:(c4 + 1) * 128], lhsT=LH[ph, c, lcol], rhs=RH[ph, c, :], start=True, stop=True),
                              ["LH", "RH"], [kk_(pt_)], acc=(c4 > 0), rg=64 * h)
                        pv = pt_[0:64, :].rearrange("p (a b) -> p a b", b=128)
                        hc = slice(h * 8 + half * 4, h * 8 + half * 4 + 4)
                        V(lambda e, pv=pv, hc=hc, o=outs[0]: e.tensor_tensor(out=o[:, hc, :], in0=pv[:, :, 0:64], in1=bcmid(mask_g[:, 0:64], 4), op=ALU.mult),
                          [kk_(pt_), "c_mask_g"], ["G_%d" % id(outs[0])])
                        V(lambda e, pv=pv, hc=hc, o=outs[1]: e.tensor_tensor(out=o[:, hc, :], in0=pv[:, :, 64:128], in1=bcmid(mask_g[:, 64:128], 4), op=ALU.mult),
                          [kk_(pt_), "c_mask_g"], ["G_%d" % id(outs[1])])
                pt_ = pP[h]
                for c in range(8):
                    P(lambda e, c=c, pt_=pt_, ph=ph: e.matmul(pt_[0:64, c * 64:(c + 1) * 64], lhsT=RH[ph, c, 0:64], rhs=LH[ph, c, 0:64], start=True, stop=True),
                      ["LH", "RH"], [kk_(pt_)], acc=(c > 0), rg=64 * h)
                V(lambda e, pt_=pt_, h=h: e.tensor_tensor(out=Nt[:, h * 8:h * 8 + 8, :], in0=pt_[0:64, :].rearrange("p (a b) -> p a b", b=64),
                                                          in1=bcmid(mask_lo[:], 8), op=ALU.mult), [kk_(pt_), "c_mask_lo"], ["G_%d" % id(Nt)])
            if X == _os.environ.get('MPASS', 'A') and int(_os.environ.get('RSTOP', '0')) == 5:
                return
            gk = lambda t: "G_%d" % id(t)
            V(lambda e: e.tensor_tensor(out=Wm[:], in0=NbA[:], in1=bcmid(identf[0:64, 0:64], 16), op=ALU.add), [gk(NbA), "c_identf"], ["Wm"])
            banks = [pA, pB, pC, pD]
            v3 = lambda t: t[0:64, :].rearrange("p (a b) -> p a b", b=64)

            def sq_mm(last):
                for hc in range(16):
                    if not last:
                        pt_ = banks[hc // 8]
                        P(lambda e, hc=hc, pt_=pt_: e.matmul(pt_[0:64, (hc % 8) * 64:(hc % 8 + 1) * 64], lhsT=Nt[:, hc, :], rhs=NbA[:, hc, :], start=True, stop=True),
                          [gk(Nt), gk(NbA)], [kk_(pt_)], acc=(hc % 8 > 0))
                    pt_ = banks[2 + hc // 8]
                    P(lambda e, hc=hc, pt_=pt_: e.matmul(pt_[0:64, (hc % 8) * 64:(hc % 8 + 1) * 64], lhsT=NbA[:, hc, :], rhs=Nt[:, hc, :], start=True, stop=True),
                      [gk(Nt), gk(NbA)], [kk_(pt_)], acc=(hc % 8 > 0))

            def sq_evac(last):
                if not last:
                    A(lambda e: e.copy(out=NbA[:, 0:8, :], in_=v3(pA)), ["pA"], [gk(NbA)])
                    A(lambda e: e.copy(out=NbA[:, 8:16, :], in_=v3(pB)), ["pB"], [gk(NbA)])
                V(lambda e: e.tensor_copy(out=Nt[:, 0:8, :], in_=v3(pC)), ["pC"], [gk(Nt)])
                V(lambda e: e.tensor_copy(out=Nt[:, 8:16, :], in_=v3(pD)), ["pD"], [gk(Nt)])

            def w_mm():
                for hc in range(16):
                    pt_ = pP[hc // 8]
                    osl = pt_[0:64, (hc % 8) * 64:(hc % 8 + 1) * 64]
                    P(lambda e, hc=hc, osl=osl: e.matmul(osl, lhsT=identb[0:64, 0:64], rhs=Wm[:, hc, :], start=True, stop=False), ["c_identb", "Wm"], [kk_(pt_)], acc=(hc % 8 > 0))
                    P(lambda e, hc=hc, osl=osl: e.matmul(osl, lhsT=Nt[:, hc, :], rhs=Wm[:, hc, :], start=False, stop=True), [gk(Nt), "Wm"], [kk_(pt_)], acc=True)

            def w_evac():
                A(lambda e: e.copy(out=Wm[:, 0:8, :], in_=v3(pP[0])), ["pP0"], ["Wm"])
                V(lambda e: e.tensor_copy(out=Wm[:, 8:16, :], in_=v3(pP[1])), ["pP1"], ["Wm"])

            sq_mm(False)
            sq_evac(False)
            for it in range(5):
                w_mm()
                if it < 4:
                    sq_mm(it + 1 == 4)
                w_evac()
                if it < 4:
                    sq_evac(it + 1 == 4)
            if X == _os.environ.get('MPASS', 'A') and int(_os.environ.get('RSTOP', '0')) == 6:
                return
            for c in range(8):
                chv = int(_os.environ.get('CHV', '0'))
                for h in range(2):
                    ph = slice(64 * h, 64 * h + 64)
                    if chv == 6 or (chv == 7 and h == 1) or (chv == 8 and h == 0) or (chv == 9 and (h == 0 or c > 0)):
                        continue
                    P(lambda e, c=c, h=h, ph=ph: e.matmul(pA[0:64, h * 64:(h + 1) * 64], lhsT=RH[ph, c, 0:64], rhs=STb[ph, hp, :], start=True, stop=True), ["RH", "STb"], ["pA"], acc=(h > 0), rg=64 * h)
                for h in range(2):
                    if chv in (5, 7, 8, 9, 10, 11):
                        continue
                    P(lambda e, c=c, h=h: e.matmul(pA[0:64, 128 + h * 64:128 + (h + 1) * 64], lhsT=Nka[:, h * 8 + c, :], rhs=Vtok[:, c, 64 * h:64 * h + 64], start=True, stop=True), [gk(Nka), "Vtok"], ["pA"], acc=True)
                if chv in (5, 6, 7, 8, 9, 10, 11):
                    continue
                A(lambda e: e.copy(out=QTf[:], in_=pA[0:64, 0:128]), ["pA"], ["QTf"])
                V(lambda e: e.tensor_tensor(out=QT[:], in0=QTf[:], in1=pA[0:64, 128:256], op=ALU.add), ["pA", "QTf"], ["QT"])
                if int(_os.environ.get('CHV', '0')) == 3:
                    continue
                for h in range(2):
                    P(lambda e, c=c, h=h: e.matmul(pB[0:64, h * 64:(h + 1) * 64], lhsT=Wm[:, h * 8 + c, :], rhs=QT[:, h * 64:(h + 1) * 64], start=True, stop=True), ["Wm", "QT"], ["pB"], acc=(h > 0))
                V(lambda e: e.tensor_copy(out=PT[:], in_=pB[0:64, 0:128]), ["pB"], ["PT"])
                if int(_os.environ.get('CHV', '0')) == 4:
                    continue
                for h in range(2):
                    if not last_pass and not store:
                        continue
                    ph = slice(64 * h, 64 * h + 64)
                    o1 = pC[0:64, (c % 4) * 128 + h * 64:(c % 4) * 128 + h * 64 + 64]
                    o2 = pD[0:64, (c % 4) * 128 + h * 64:(c % 4) * 128 + h * 64 + 64]
                    P(lambda e, c=c, h=h, ph=ph, o1=o1: e.matmul(o1, lhsT=RH[ph, c, 64:128], rhs=STb[ph, hp, :], start=True, stop=True), ["RH", "STb"], ["pC"], acc=not (c % 4 == 0 and h == 0), rg=64 * h)
                    P(lambda e, c=c, h=h, o2=o2: e.matmul(o2, lhsT=Mbr[:, h * 8 + c, :], rhs=PT[:, h * 64:(h + 1) * 64], start=True, stop=False), [gk(Mbr), "PT"], ["pD"], acc=not (c % 4 == 0 and h == 0))
                    P(lambda e, c=c, h=h, o2=o2: e.matmul(o2, lhsT=Mkr[:, h * 8 + c, :], rhs=Vtok[:, c, 64 * h:64 * h + 64], start=False, stop=True), [gk(Mkr), "Vtok"], ["pD"], acc=True)
                if c % 4 == 3 and (last_pass or store):
                    A(lambda e, c=c: e.copy(out=YT[:, c - 3:c + 1, :], in_=pC[0:64, :].rearrange("p (a b) -> p a b", b=128)), ["pC"], ["YT"])
                    V(lambda e, c=c: e.tensor_tensor(out=YT[:, c - 3:c + 1, :], in0=YT[:, c - 3:c + 1, :], in1=pD[0:64, :].rearrange("p (a b) -> p a b", b=128), op=ALU.add), ["pD", "YT"], ["YT"])
                pS = pP[c % 2]
                if int(_os.environ.get('CHV', '0')) != 2:
                    P(lambda e, c=c, pS=pS: e.matmul(pS[:, 0:128], lhsT=Btok[:, c, :], rhs=PT[:], start=True, stop=False), ["Btok", "PT"], [kk_(pS)])
                    P(lambda e, c=c, pS=pS: e.matmul(pS[:, 0:128], lhsT=Ktok[:, c, :], rhs=Vtok[:, c, :], start=False, stop=True), ["Ktok", "Vtok"], [kk_(pS)], acc=True)
                for h in range(2):
                    if int(_os.environ.get('CHV', '0')) in (1, 2):
                        continue
                    ph = slice(64 * h, 64 * h + 64)
                    V(lambda e, ph=ph, h=h, pS=pS: e.tensor_tensor(out=ST[ph, hp, :], in0=ST[ph, hp, :], in1=pS[ph, 64 * h:64 * h + 64], op=ALU.add), ["ST", kk_(pS)], ["ST"])
                    V(lambda e, ph=ph, c=c: e.tensor_scalar(out=ST[ph, hp, :], in0=ST[ph, hp, :], scalar1=arena[ph, EP_, 64 * c + 63:64 * c + 64], scalar2=None, op0=ALU.mult), ["ST"] + wk(EP_), ["ST"])
                G(lambda e: e.tensor_copy(out=STb[:, hp, :], in_=ST[:, hp, :]), ["ST"], ["STb"])
            if X == _os.environ.get('MPASS', 'A') and int(_os.environ.get('RSTOP', '0')) == 7:
                return
            if not last_pass and not store:
                return
            for c in range(8):
                P(lambda e, c=c: e.transpose(out=pA[:, c * 64:(c + 1) * 64], in_=YT[:, c, :], identity=identf[0:64, 0:64]), ["YT", "c_identf"], ["pA"], acc=(c > 0))
            YR = R_
            A(lambda e: e.copy(out=W(YR), in_=pA[:]), ["pA"], wk(YR))
            if not last_pass:
                if store:
                    LD(lambda e: e.dma_start(out=Y1[j, 1024 + hp * 128:1024 + (hp + 1) * 128, :], in_=W(YR)), wk(YR), ["Y1_%d" % j])
                return
            jm = NT - 1 - j
            LD(lambda e: e.dma_start(out=W(K_), in_=Y1[jm, 1024 + hp * 128:1024 + (hp + 1) * 128, :]), ["Y1_%d" % jm], wk(K_))
            rev = bass.AP(arena[:].tensor, arena[:, K_, 511:512].offset, [list(arena[:].ap[0]), [-1, 512]])
            V(lambda e: e.tensor_tensor(out=W(0), in0=W(YR), in1=rev, op=ALU.add), wk(YR, K_), wk(0))
            P(lambda e: e.matmul(pB[:], lhsT=bones[:], rhs=W(0), start=True, stop=True), ["c_bones"] + wk(0), ["pB"])
            V(lambda e: e.scalar_tensor_tensor(out=W(0), in0=pB[:], scalar=-1.0 / 64, in1=W(0), op0=ALU.mult, op1=ALU.add), ["pB"] + wk(0), wk(0))
            A(lambda e: e.activation(out=W(1), in_=W(0), func=AF.Square), wk(0), wk(1))
            P(lambda e: e.matmul(pC[:], lhsT=bones[:], rhs=W(1), start=True, stop=True), ["c_bones"] + wk(1), ["pC"])
            V(lambda e: e.tensor_scalar(out=W(1), in0=pC[:], scalar1=1.0 / 64, scalar2=64e-5, op0=ALU.mult, op1=ALU.add), ["pC"], wk(1))
            A(lambda e: e.activation(out=W(1), in_=W(1), func=AF.Sqrt), wk(1), wk(1))
            V(lambda e: e.reciprocal(out=W(1), in_=W(1)), wk(1), wk(1))
            V(lambda e: e.tensor_tensor(out=W(0), in0=W(0), in1=W(1), op=ALU.mult), wk(0, 1), wk(0))
            V(lambda e: e.tensor_scalar(out=W(0), in0=W(0), scalar1=col(vec["ln_w"]), scalar2=col(vec["ln_b"]), op0=ALU.mult, op1=ALU.add), wk(0) + ["c_ln_w", "c_ln_b"], wk(0))
            V(lambda e: e.tensor_tensor(out=W(0), in0=W(0), in1=W(BON_), op=ALU.add), wk(0, BON_), wk(0))
            pt, pk = proj(42 + hp, False)
            A(lambda e: e.activation(out=W(1), in_=pt[:], func=AF.Silu), [pk], wk(1))
            V(lambda e: e.tensor_tensor(out=yrwb[:, hp, :], in0=W(0), in1=W(1), op=ALU.mult), wk(0, 1), ["yrwb"])

        glu_v = dr["w_glu"].rearrange("(kc p) m -> p kc m", p=128)
        s5o_v = dr["s5_w_out"].rearrange("(kc p) m -> p kc m", p=128)
        rwo_v = dr["rw_w_out"].rearrange("(kc p) m -> p kc m", p=128)
        wo_v = dr["w_o"].rearrange("(kc p) m -> p kc m", p=128)
        aflat = arena[:].rearrange("p a b -> p (a b)")
        ytok = aflat[:, 0:4096].rearrange("p (a b) -> p a b", b=2048)
        stat2 = sb([128, 8])

        def stage_tail(xsrc, j):
            for cb in range(8):
                wv, wkey = load_wb("glu", cb, 8, 128)
                for kc in range(8):
                    P(lambda e, kc=kc: e.matmul(pA[:], lhsT=wv[:, kc, :], rhs=zb[:, kc, :], start=(kc == 0), stop=(kc == 7)), [wkey, "zb"], ["pA"], acc=(kc > 0))
                A(lambda e: e.activation(out=W(0), in_=pA[:], func=AF.Sigmoid), ["pA"], wk(0))
                pt, pk = proj(8 + cb, False)
                A(lambda e: e.activation(out=W(1), in_=pt[:], func=AF.Silu), [pk], wk(1))
                V(lambda e: e.tensor_tensor(out=W(0), in0=W(0), in1=W(1), op=ALU.mult), wk(0, 1), wk(0))
                V(lambda e, cb=cb: e.tensor_tensor(out=zzb[:, cb, :], in0=zb[:, cb, :], in1=W(0), op=ALU.mult), wk(0) + ["zb"], ["zzb"])
            if int(_os.environ.get('TSTOP', '0')) == 1:
                return
            for ob in range(16):
                wv, wkey = load_wb("s5o", ob, 8, 128)
                for kc in range(8):
                    P(lambda e, kc=kc: e.matmul(pA[:], lhsT=wv[:, kc, :], rhs=zzb[:, kc, :], start=(kc == 0), stop=(kc == 7)), [wkey, "zzb"], ["pA"], acc=(kc > 0))
                wv2, wkey2 = load_wb("rwo", ob, 8, 128)
                for kc in range(8):
                    P(lambda e, kc=kc: e.matmul(pB[:], lhsT=wv2[:, kc, :], rhs=yrwb[:, kc, :], start=(kc == 0), stop=(kc == 7)), [wkey2, "yrwb"], ["pB"], acc=(kc > 0))
                pt0, pk0 = proj(50 + ob, False)
                A(lambda e: e.activation(out=W(8), in_=pt0[:], func=AF.Sigmoid), [pk0], wk(8))
                pt1, pk1 = proj(66 + ob, False)
                A(lambda e: e.activation(out=W(9), in_=pt1[:], func=AF.Sigmoid), [pk1], wk(9))
                V(lambda e: e.tensor_tensor(out=W(8), in0=W(8), in1=pA[:], op=ALU.mult), wk(8) + ["pA"], wk(8))
                V(lambda e: e.tensor_tensor(out=W(9), in0=W(9), in1=pB[:], op=ALU.mult), wk(9) + ["pB"], wk(9))
                G(lambda e, ob=ob: e.tensor_tensor(out=hb[:, ob, :], in0=W(8), in1=W(9), op=ALU.add), wk(8, 9), ["hb"])
            if int(_os.environ.get('TSTOP', '0')) == 2:
                return
            yk = wk(*range(8))
            for tqp in range(2):
                V(lambda e: e.memset(stat2[:], 0.0), [], ["stat2"])
                for nb in range(4):
                    LD(lambda e, nb=nb: e.dma_start(out=gsl, in_=dr["post_g"][:, nb * 512:(nb + 1) * 512]), [], ["s5t"])
                    for kq in range(4):
                        wv, wkey = load_wb("wo", nb * 4 + kq, 4, 512)
                        for t2 in range(2):
                            tq = tqp * 2 + t2
                            pX = (pC, pD)[t2]
                            for k4 in range(4):
                                P(lambda e, k4=k4, kq=kq, tq=tq, pX=pX: e.matmul(pX[:], lhsT=hb[:, kq * 4 + k4, tq * 128:(tq + 1) * 128], rhs=wv[:, k4, :],
                                                                          start=(kq == 0 and k4 == 0), stop=(kq == 3 and k4 == 3)),
                                  [wkey, "hb"], [kk_(pX)], acc=not (kq == 0 and k4 == 0))
                    if int(_os.environ.get('TSTOP', '0')) == 3:
                        return
                    for t2 in range(2):
                        pX = (pC, pD)[t2]
                        if int(_os.environ.get('TSKIP', '0')) != 1:
                            A(lambda e, t2=t2, nb=nb, pX=pX: e.activation(out=xn[:, 0:512], in_=pX[:], func=AF.Square, accum_out=stat2[:, t2 * 4 + nb:t2 * 4 + nb + 1]),
                              [kk_(pX), "stat2"], ["xn", "stat2"])
                        if int(_os.environ.get('TSKIP', '0')) != 2:
                            V(lambda e, t2=t2, nb=nb, pX=pX: e.tensor_tensor(out=ytok[:, t2, nb * 512:(nb + 1) * 512], in0=pX[:], in1=gsl, op=ALU.mult),
                              [kk_(pX), "s5t", "xn"], yk)
                if int(_os.environ.get('TSTOP', '0')) == 4:
                    return
                for t2 in range(2):
                    tq = tqp * 2 + t2
                    V(lambda e, t2=t2: e.tensor_reduce(out=stat[:, 0:1], in_=stat2[:, t2 * 4:(t2 + 1) * 4], axis=mybir.AxisListType.X, op=ALU.add), ["stat2"], ["stat"])
                    V(lambda e: e.tensor_scalar(out=stat[:, 1:2], in0=stat[:, 0:1], scalar1=1.0 / D, scalar2=1e-6, op0=ALU.mult, op1=ALU.add), ["stat"], ["stat"])
                    A(lambda e: e.activation(out=stat[:, 1:2], in_=stat[:, 1:2], func=AF.Sqrt), ["stat"], ["stat"])
                    V(lambda e: e.reciprocal(out=stat[:, 2:3], in_=stat[:, 1:2]), ["stat"], ["stat"])
                    LD(lambda e, tq=tq: e.dma_start(out=xs[:, :], in_=xsrc[j, 1 + 128 * tq:129 + 128 * tq, :]), [], ["xs"])
                    V(lambda e, t2=t2: e.scalar_tensor_tensor(out=ytok[:, t2, :], in0=ytok[:, t2, :], scalar=stat[:, 2:3], in1=xs[:, :], op0=ALU.mult, op1=ALU.add),
                      yk + ["stat", "xs"], yk)
                    LD(lambda e, t2=t2, tq=tq: e.dma_start(out=yB[j * TT + tq * 128:j * TT + (tq + 1) * 128, :], in_=ytok[:, t2, :]), yk, ["yB"])

        stf = ST[:].rearrange("p a b -> p (a b)")
        xsel = sb([128, 8])
        LD(lambda e: e.dma_start(out=xsel[:], in_=dr["xsel"]), [], ["c_xsel"])
        XW = 8 * 64 + 128
        xst_out = nc.dram_tensor("xst_out", [128, XW], F32, kind="Internal").ap()

        def save_state():
            LD(lambda e: e.dma_start(out=xst_out[:, 0:512], in_=stf), ["ST"], ["xst_out"])
            LD(lambda e: e.dma_start(out=xst_out[:, 512:576], in_=X1c[:]), ["X1c"], ["xst_out"])
            LD(lambda e: e.dma_start(out=xst_out[:, 576:640], in_=X2c[:]), ["X2c"], ["xst_out"])

        def restore_state():
            LD(lambda e: e.dma_start(out=stf, in_=xst_out[:, 0:512]), ["xst_out"], ["ST"])
            LD(lambda e: e.dma_start(out=X1c[:], in_=xst_out[:, 512:576]), ["xst_out"], ["X1c"])
            LD(lambda e: e.dma_start(out=X2c[:], in_=xst_out[:, 576:640]), ["xst_out"], ["X2c"])

        for cb in range(82):
            precast("in", cb, w_in_v[:, :, cb * 128:(cb + 1) * 128], 16, 128)
        for cb in range(8):
            precast("glu", cb, glu_v[:, :, cb * 128:(cb + 1) * 128], 8, 128)
        for ob in range(16):
            precast("s5o", ob, s5o_v[:, :, ob * 128:(ob + 1) * 128], 8, 128)
            precast("rwo", ob, rwo_v[:, :, ob * 128:(ob + 1) * 128], 8, 128)
        for nb in range(4):
            for kq in range(4):
                precast("wo", nb * 4 + kq, wo_v[:, kq * 4:(kq + 1) * 4, nb * 512:(nb + 1) * 512], 4, 512)

        plist = [("P", "B", "xP", False, False), ("A", "A", "xA", False, True), ("B", "B", "xB", True, True)]
        plist = [p for p in plist if p[0] in passes]
        for pi3, (pname, X, xname, last, store) in enumerate(plist):
            pi = ("P", "A", "B").index(pname)
            xsrc = dr[xname]
            if pname == "A" and "P" in passes:
                save_state()
            if pname == "B" and "P" in passes:
                restore_state()
            for j in range(NT):
                cc = cm[:, pi * NT + j:pi * NT + j + 1]
                V(lambda e, cc=cc: e.tensor_scalar(out=stf, in0=stf, scalar1=cc, scalar2=None, op0=ALU.mult), ["ST", "c_cm"], ["ST"])
                V(lambda e: e.tensor_copy(out=STb[:].rearrange("p a b -> p (a b)"), in_=stf), ["ST"], ["STb"])
                V(lambda e, cc=cc: e.tensor_scalar(out=X1c[:], in0=X1c[:], scalar1=cc, scalar2=None, op0=ALU.mult), ["X1c", "c_cm"], ["X1c"])
                V(lambda e, cc=cc: e.tensor_scalar(out=X2c[:], in0=X2c[:], scalar1=cc, scalar2=None, op0=ALU.mult), ["X2c", "c_cm"], ["X2c"])
                ms = int(_os.environ.get('MSTOP', '0')) if X == _os.environ.get('MPASS', 'A') else 0
                if ms == 1:
                    break
                stage_load(xsrc, j)
                if ms == 2:
                    break
                for cb in range(8):
                    stage_s5(X, j, cb, last, store)
                    if ms == 3:
                        break
                if ms in (3, 4):
                    break
                stage_lora(X)
                if ms == 5:
                    break
                for hp in range(8):
                    stage_rwkv(X, j, hp, last, store)
                    if ms == 6:
                        break
                if ms in (6, 7):
                    break
                if last:
                    stage_tail(xsrc, j)
            if int(_os.environ.get('MSTOP', '0')) > 0 and X == _os.environ.get('MPASS', 'A'):
                break
        S.finish("sp")
    S.close()
    return nc, S


def kernel(**inputs):
    runs, NT = make_plan("bal")
    partners = [1, 0, 3, 2, None, None, None, None]
    in_maps, metas, ntok = host_layout(inputs, runs, NT, partners)
    nc, _ = build_program(NT, n_cores=8)
    res = run_bass_kernel_spmd(nc, in_maps, core_ids=list(range(8)))
    y_all = np.zeros((ntok, D), np.float32)
    for c, r in enumerate(res.results):
        yb = np.asarray(r["yB"])
        gi = metas[c]["gidxB"]
        valid = gi < ntok
        y_all[gi[valid]] = yb[valid]
    return (y_all[:32768].reshape(2, 16384, D), y_all[32768:].reshape(16, 2048, D))
```

```python
import math
import numpy as np
from contextlib import ExitStack
import concourse.bass as bass
import concourse.mybir as mybir
from concourse.bass_utils import run_bass_kernel_spmd

F32 = mybir.dt.float32
BF16 = mybir.dt.bfloat16
AF = mybir.ActivationFunctionType
ALU = mybir.AluOpType

D = 2048
TT = 512
CT = 64
NCK = TT // CT
NBK = TT // 8
N_IN = 10496
PI = math.pi
TWO_PI = 2.0 * math.pi
import os as _os
_STOP = int(_os.environ.get('S5STOP', '0'))


class Sched:
    def __init__(self, nc, same_engine_sync=True):
        self.nc = nc
        self.same = same_engine_sync
        self.eng = {"pe": nc.tensor, "act": nc.scalar, "dve": nc.vector, "pool": nc.gpsimd, "sp": nc.sync}
        self.streams = {}
        for s, e, inc in [("pe", "pe", 1), ("act", "act", 1), ("dve", "dve", 1), ("pool", "pool", 1)]:
            self.streams[s] = dict(eng=e, inc=inc, sem=None, cnt=0)
        self.dma_q = {"dsp": ("sp", 14), "dpool": ("pool", 10)}
        self.rr = {q: 0 for q in self.dma_q}
        for q, (e, n) in self.dma_q.items():
            for i in range(n):
                self.streams["%s%d" % (q, i)] = dict(eng=e, inc=16, sem=None, cnt=0)
        self.seen = {e: {s: 0 for s in self.streams} for e in self.eng}
        self.lastw = {}
        self.readers = {}
        self._ctx = []
        self.n_wait = 0
        self.n_ins = 0

    def open(self):
        for s, d in self.streams.items():
            cm = self.nc.semaphore("sem_" + s)
            d["sem"] = cm.__enter__()
            self._ctx.append(cm)

    def close(self):
        for cm in reversed(self._ctx):
            cm.__exit__(None, None, None)

    def _wait(self, e, s, c):
        if c <= 0 or self.seen[e][s] >= c:
            return
        d = self.streams[s]
        self.eng[e].wait_ge(d["sem"], c * d["inc"])
        self.seen[e][s] = c
        self.n_wait += 1

    def emit(self, stream, fn, reads=(), writes=(), acc=False, rg=0):
        if stream in self.dma_q:
            q = stream
            i = self.rr[q]
            self.rr[q] = (i + 1) % self.dma_q[q][1]
            stream = "%s%d" % (q, i)
            d = self.streams[stream]
            self._wait(d["eng"], stream, d["cnt"])
        d = self.streams[stream]
        e = d["eng"]
        deps = set()
        raw = set()
        for k in reads:
            if k in self.lastw:
                deps.add(self.lastw[k])
                raw.add(self.lastw[k])
        for k in writes:
            if k in self.lastw:
                deps.add(self.lastw[k])
            for r in self.readers.get(k, ()):
                deps.add(r)
        for (s2, c2) in deps:
            if s2 == stream and d["inc"] == 1:
                if stream == "pe" or not self.same or (s2, c2) not in raw:
                    continue
            self._wait(e, s2, c2)
        if stream == "pe":
            if rg != getattr(self, "last_rg", 0):
                self._wait(e, "pe", d["cnt"])
            self.last_rg = rg
        ins = fn(self.eng[e])
        d["cnt"] += 1
        ins.then_inc(d["sem"], d["inc"])
        me = (stream, d["cnt"])
        for k in writes:
            self.lastw[k] = me
            self.readers[k] = set()
        for k in reads:
            self.readers.setdefault(k, set()).add(me)
        self.n_ins += 1
        return ins

    def barrier(self):
        for e in self.eng:
            for s, d in self.streams.items():
                self._wait(e, s, d["cnt"])
        self.lastw = {}
        self.readers = {}

    def finish(self, e="sp"):
        for s, d in self.streams.items():
            self._wait(e, s, d["cnt"])


def _col(v):
    v = np.asarray(v, np.float32).reshape(-1, 128)
    return np.ascontiguousarray(v.T)


def _constants():
    c = {}
    c["identf"] = np.eye(128, dtype=np.float32)
    psw = np.zeros((128, 128), np.float32)
    for p in range(64):
        psw[p, 64 + p] = 1.0
        psw[64 + p, p] = 1.0
    c["pswap"] = psw
    bo = np.zeros((128, 128), np.float32)
    bo[:64, :64] = 1.0
    bo[64:, 64:] = 1.0
    c["bones"] = bo
    sp = np.zeros((128, 8, 240), np.float32)
    for g8 in range(8):
        for h in range(16):
            sp[g8 * 16 + h, g8, 112 + h] = 1.0
    c["selpad"] = sp
    c["seltpad"] = sp.copy()
    s_idx = np.arange(64)[:, None]
    t_idx = np.arange(64)[None, :]
    su = (s_idx < t_idx).astype(np.float32)
    ui = (s_idx <= t_idx).astype(np.float32)
    c["mask_g"] = np.concatenate([su, ui], axis=1)
    c["mask_lo"] = (s_idx > t_idx).astype(np.float32)
    rst = np.ones((128, TT), np.float32)
    rst[:, ::CT] = 0.0
    c["rst"] = rst
    c["ramp"] = np.tile(np.arange(1, 65, dtype=np.float32)[None, :], (128, 1))
    sg = np.ones((128, 1), np.float32)
    sg[:64] = -1.0
    c["sgna"] = sg
    return c


CONST_SHAPES = {"identf": [128, 128], "pswap": [128, 128], "bones": [128, 128], "selpad": [128, 8, 240],
                "seltpad": [128, 8, 240], "mask_g": [64, 128], "mask_lo": [64, 64], "rst": [128, TT],
                "ramp": [128, 64], "sgna": [128, 1]}


def _seq_bounds(g):
    if g < 32768:
        s = (g // 16384) * 16384
        return s, s + 16384
    s = 32768 + ((g - 32768) // 2048) * 2048
    return s, s + 2048


def make_plan(mode):
    if mode == "v1":
        runs = [[(0, 16384, 'f')], [(16384, 16384, 'f')]]
        groups = [[0, 1, 2], [3, 4, 5], [6, 7, 8], [9, 10, 11], [12, 13], [14, 15]]
        for gsel in groups:
            runs.append([(32768 + s * 2048, 2048, 'f') for s in gsel])
        return runs, 32
    if mode == "bal":
        runs = [[(0, 8192, 'f')], [(8192, 8192, 'r')], [(16384, 8192, 'f')], [(24576, 8192, 'r')]]
        for c in range(4):
            runs.append([(32768 + (4 * c + i) * 2048, 2048, 'f') for i in range(4)])
        return runs, 16
    raise ValueError(mode)


def host_layout(inputs, runs, NT, partners=None):
    f32 = np.float32
    xp = np.asarray(inputs["x_prompt"], f32).reshape(-1, D)
    xs = np.asarray(inputs["x_sample"], f32).reshape(-1, D)
    ntok = xp.shape[0] + xs.shape[0]
    x_pad = np.concatenate([xp, xs, np.zeros((1, D), f32)], axis=0)
    ZERO = ntok
    consts = _constants()
    g = lambda k: np.asarray(inputs[k], f32)[0]
    shared = {}
    shared["w_glu"] = g("s5_w_glu")
    shared["s5_w_out"] = g("s5_w_out")
    shared["rw_w_out"] = g("rw_w_out")
    shared["w_o"] = g("w_o")
    shared["pre_g"] = _col(g("pre_norm_g"))
    shared["post_g"] = np.ascontiguousarray(np.tile(g("post_norm_g")[None, :], (128, 1)))
    shared["k_k"] = _col(g("rw_k_k"))
    shared["k_a"] = _col(g("rw_k_a"))
    shared["r_k"] = _col(g("rw_r_k").reshape(-1))
    shared["ln_w"] = _col(g("rw_ln_w"))
    shared["ln_b"] = _col(g("rw_ln_b"))
    d = g("s5_d").reshape(64, 16)
    shared["d8"] = np.ascontiguousarray(np.tile(d.T, (8, 1)))
    shared.update(consts)

    def s5dir(dd, sfx):
        o = {}
        lre = g("s5_lam_re")[dd].T
        lim = g("s5_lam_im")[dd].T
        o["lamre" + sfx] = np.ascontiguousarray(np.concatenate([lre, lre], 0))
        o["lamim" + sfx] = np.ascontiguousarray(np.concatenate([lim, lim], 0))
        o["logdt" + sfx] = np.ascontiguousarray(np.tile(g("s5_log_dt")[dd][None, :], (128, 1)))
        bre = g("s5_b_re")[dd].transpose(1, 0, 2)
        bim = g("s5_b_im")[dd].transpose(1, 0, 2)
        o["bst" + sfx] = np.ascontiguousarray(np.concatenate([bre, bim], 0))
        o["bsw" + sfx] = np.ascontiguousarray(np.concatenate([bim, bre], 0))
        cre = g("s5_c_re")[dd].transpose(2, 0, 1)
        cim = g("s5_c_im")[dd].transpose(2, 0, 1)
        o["cst" + sfx] = np.ascontiguousarray(np.concatenate([cre, cim], 0))
        o["csw" + sfx] = np.ascontiguousarray(np.concatenate([cim, cre], 0))
        o["w0" + sfx] = _col(g("rw_w0")[dd])
        o["a0" + sfx] = _col(g("rw_a0")[dd])
        o["wup" + sfx] = np.ascontiguousarray(g("rw_w_up")[dd])
        o["aup" + sfx] = np.ascontiguousarray(g("rw_a_up")[dd])
        return o

    dir_cache = {}
    w_in_cache = {}
    w_in = g("w_in")
    mu = g("rw_mu")
    in_maps = []
    metas = []
    for cr in runs:
        dirA = 0 if cr[0][2] == 'f' else 1
        assert all((r[2] == 'f') == (dirA == 0) for r in cr)
        gidx = []
        runid = []
        for ri, (st, ln, dr) in enumerate(cr):
            ids = list(range(st, st + ln))
            if dr == 'r':
                ids = ids[::-1]
            gidx += ids
            runid += [ri] * ln
        assert len(gidx) % TT == 0 and len(gidx) <= NT * TT
        npad = NT * TT - len(gidx)
        gidx += [ZERO] * npad
        runid += [-1] * npad
        gidx = np.array(gidx, np.int64)
        runid = np.array(runid, np.int64)
        step = 1 if dirA == 0 else -1
        idxA = np.full((NT, TT + 2), ZERO, np.int64)
        for j in range(NT):
            ids = gidx[j * TT:(j + 1) * TT]
            idxA[j, 1:TT + 1] = ids
            if ids[0] != ZERO:
                lo, hi = _seq_bounds(int(ids[0]))
                b = int(ids[0]) - step
                if lo <= b < hi:
                    idxA[j, 0] = b
                a = int(ids[-1]) + step
                if lo <= a < hi:
                    idxA[j, TT + 1] = a
        xA = x_pad[idxA]
        xB = np.ascontiguousarray(xA[::-1, ::-1, :])
        tile_run = runid[::TT]
        cm = np.zeros((2 * NT,), f32)
        for j in range(1, NT):
            if tile_run[j] >= 0 and tile_run[j] == tile_run[j - 1]:
                cm[j] = 1.0
        for j in range(1, NT):
            a, b = tile_run[NT - 1 - j], tile_run[NT - j]
            if a >= 0 and a == b:
                cm[NT + j] = 1.0
        m = dict(shared)
        m["xA"] = np.ascontiguousarray(xA)
        m["xB"] = xB
        cm[NT] = 1.0
        m["cm_ab"] = cm
        xsel = np.zeros((128, 8), f32)
        ci = len(in_maps)
        if partners is not None and partners[ci] is not None:
            xsel[:, partners[ci]] = 1.0
        m["xsel"] = xsel
        if dirA not in dir_cache:
            oa = s5dir(dirA, "A")
            ob = s5dir(1 - dirA, "B")
            oa.update(ob)
            dir_cache[dirA] = oa
            if dirA == 0:
                w_in_cache[dirA] = (w_in, _col(mu))
            else:
                perm = np.arange(N_IN)
                base = 2048 + 3072
                perm[base:base + 256] = base + np.array(list(range(64, 128)) + list(range(0, 64)) +
                                                       list(range(192, 256)) + list(range(128, 192)))
                mperm = np.arange(3328)
                mperm[3072:] = perm[base:base + 256] - 2048
                w_in_cache[dirA] = (np.ascontiguousarray(w_in[:, perm]), _col(mu[mperm]))
        m.update(dir_cache[dirA])
        m["w_in"], m["mu"] = w_in_cache[dirA]
        in_maps.append(m)
        metas.append(dict(gidxB=gidx[::-1].copy()))
    zeroP = None
    for ci, m in enumerate(in_maps):
        pr = partners[ci] if partners is not None else None
        if pr is None:
            if zeroP is None:
                zeroP = np.zeros_like(m["xA"])
            m["xP"] = zeroP
            cmp_ = np.zeros((NT,), f32)
        else:
            m["xP"] = in_maps[pr]["xA"]
            cmp_ = in_maps[pr]["cm_ab"][:NT]
        m["cm"] = np.ascontiguousarray(np.tile(np.concatenate([cmp_, m["cm_ab"]])[None, :], (128, 1)))
    for m in in_maps:
        del m["cm_ab"]
    return in_maps, metas, ntok


def input_shapes(NT):
    sh = {"xA": [NT, TT + 2, D], "xB": [NT, TT + 2, D], "xP": [NT, TT + 2, D], "cm": [128, 3 * NT], "xsel": [128, 8],
          "w_in": [D, N_IN], "w_glu": [1024, 1024], "s5_w_out": [1024, D], "rw_w_out": [1024, D], "w_o": [D, D],
          "pre_g": [128, 16], "post_g": [128, D], "mu": [128, 26], "k_k": [128, 8], "k_a": [128, 8],
          "r_k": [128, 8], "ln_w": [128, 8], "ln_b": [128, 8], "d8": [128, 64]}
    sh.update(CONST_SHAPES)
    for X in "AB":
        for k in ("lamre", "lamim", "logdt"):
            sh[k + X] = [128, 64]
        for k in ("bst", "bsw", "cst", "csw"):
            sh[k + X] = [128, 64, 16]
        sh["w0" + X] = [128, 8]
        sh["a0" + X] = [128, 8]
        sh["wup" + X] = [64, 1024]
        sh["aup" + X] = [64, 1024]
    return sh


def bc(ap2, n):
    sh = list(ap2.shape)
    return ap2.unsqueeze(len(sh)).broadcast_to(sh + [n])


def bcmid(ap2, n):
    sh = list(ap2.shape)
    return ap2.unsqueeze(1).broadcast_to([sh[0], n] + sh[1:])


def gen_s5(nc, S, dr, X, add_d, S5W, S5T, r8_out):
    with ExitStack() as es:
        cnt = [0]

        def sb(shape, dt=F32):
            cnt[0] += 1
            return es.enter_context(nc.sbuf_tensor("g%s_%d" % (X, cnt[0]), shape, dt))

        def ps(shape, dt=F32):
            cnt[0] += 1
            return es.enter_context(nc.psum_tensor("gp%s_%d" % (X, cnt[0]), shape, dt))

        K = lambda n: "g" + X + n
        lamre = sb([128, 64]); lamim = sb([128, 64]); dtt = sb([128, 64]); aa = sb([128, 64]); phi = sb([128, 64])
        ER = sb([128, 9, 64]); EI = sb([128, 9, 64])
        t1 = sb([128, 64]); t2 = sb([128, 64]); t3 = sb([128, 64]); t4 = sb([128, 64])
        fre = sb([128, 64]); fs = sb([128, 64]); nfs = sb([128, 64])
        bst = sb([128, 64, 16]); bsw = sb([128, 64, 16]); cst = sb([128, 64, 16]); csw = sb([128, 64, 16])
        bbst = sb([128, 64, 16]); bbsw = sb([128, 64, 16]); u1 = sb([128, 64, 16]); u2 = sb([128, 64, 16])
        LB = sb([128, 64, 128]); CL = sb([128, 64, 256])
        bpad = [sb([128, 8, 128]), sb([128, 8, 128])]
        w5 = [sb([128, 4, 128], BF16), sb([128, 4, 128], BF16)]
        T0 = sb([128, 64, 64]); T1 = sb([128, 64, 64])
        identf = sb([128, 128]); pswap = sb([128, 128]); d8 = sb([128, 64]); ramp = sb([128, 64])
        sgna = sb([128, 1]); nsgna = sb([128, 1]); negpi = sb([128, 1]); th8 = sb([128, 64])
        psW = [ps([128, 512]), ps([128, 512])]
        ti64 = sb([128, 64], mybir.dt.int32); tiT = sb([128, 64, 64], mybir.dt.int32)

        def ld(t, name, key):
            S.emit("dsp", lambda e: e.dma_start(out=t[:], in_=dr[name]), writes=[K(key)])

        ld(lamre, "lamre" + X, "lamre"); ld(lamim, "lamim" + X, "lamim"); ld(dtt, "logdt" + X, "dt")
        ld(bst, "bst" + X, "bst"); ld(bsw, "bsw" + X, "bsw"); ld(cst, "cst" + X, "cst"); ld(csw, "csw" + X, "csw")
        ld(identf, "identf", "identf"); ld(pswap, "pswap", "pswap"); ld(d8, "d8", "d8"); ld(ramp, "ramp", "ramp")
        ld(sgna, "sgna", "sgna")
        V = lambda fn, r, w: S.emit("dve", fn, reads=[K(x) for x in r], writes=[K(x) for x in w])
        A = lambda fn, r, w: S.emit("act", fn, reads=[K(x) for x in r], writes=[K(x) for x in w])
        G = lambda fn, r, w: S.emit("pool", fn, reads=[K(x) for x in r], writes=[K(x) for x in w])

        def sin_reduced(dst, src, mul, add, tf, ti, dkey, rkeys):
            V(lambda e: e.tensor_scalar(out=dst[:], in0=src, scalar1=mul, scalar2=add, op0=ALU.mult, op1=ALU.add), rkeys, [dkey])
            V(lambda e: e.tensor_scalar(out=tf[:], in0=dst[:], scalar1=1.0 / TWO_PI, scalar2=None, op0=ALU.mult), [dkey], ["rtf"])
            V(lambda e: e.tensor_copy(out=ti[:], in_=tf[:]), ["rtf"], ["rti"])
            V(lambda e: e.tensor_copy(out=tf[:], in_=ti[:]), ["rti"], ["rtf"])
            V(lambda e: e.scalar_tensor_tensor(out=dst[:], in0=tf[:], scalar=-TWO_PI, in1=dst[:], op0=ALU.mult, op1=ALU.add), ["rtf", dkey], [dkey])
            V(lambda e: e.tensor_scalar(out=dst[:], in0=dst[:], scalar1=-PI, scalar2=PI, op0=ALU.max, op1=ALU.min), [dkey], [dkey])
            A(lambda e: e.activation(out=dst[:], in_=dst[:], func=AF.Sin), [dkey], [dkey])

        V(lambda e: e.memset(negpi[:], -PI), [], ["negpi"])
        V(lambda e: e.tensor_scalar(out=nsgna[:], in0=sgna[:], scalar1=-1.0, scalar2=None, op0=ALU.mult), ["sgna"], ["nsgna"])
        A(lambda e: e.activation(out=dtt[:], in_=dtt[:], func=AF.Exp), ["dt"], ["dt"])
        V(lambda e: e.tensor_tensor(out=aa[:], in0=lamre[:], in1=dtt[:], op=ALU.mult), ["lamre", "dt"], ["aa"])
        V(lambda e: e.tensor_tensor(out=phi[:], in0=lamim[:], in1=dtt[:], op=ALU.mult), ["lamim", "dt"], ["phi"])
        for j in range(9):
            A(lambda e, j=j: e.activation(out=t1[:], in_=aa[:], func=AF.Exp, scale=float(j)), ["aa"], ["t1"])
            sin_reduced(t3, phi[:], float(j), 0.0, t2, ti64, "t3", ["phi"])
            V(lambda e, j=j: e.tensor_tensor(out=EI[:, j, :], in0=t1[:], in1=t3[:], op=ALU.mult), ["t1", "t3"], ["EI%d" % j])
            sin_reduced(t4, phi[:], float(j), 0.5 * PI, t2, ti64, "t4", ["phi"])
            V(lambda e, j=j: e.tensor_tensor(out=ER[:, j, :], in0=t1[:], in1=t4[:], op=ALU.mult), ["t1", "t4"], ["ER%d" % j])
        if _STOP == 2:
            S.barrier(); return
        V(lambda e: e.tensor_tensor(out=t1[:], in0=lamre[:], in1=lamre[:], op=ALU.mult), ["lamre"], ["t1"])
        V(lambda e: e.tensor_tensor(out=t2[:], in0=lamim[:], in1=lamim[:], op=ALU.mult), ["lamim"], ["t2"])
        V(lambda e: e.tensor_tensor(out=t1[:], in0=t1[:], in1=t2[:], op=ALU.add), ["t1", "t2"], ["t1"])
        V(lambda e: e.reciprocal(out=t1[:], in_=t1[:]), ["t1"], ["t1"])
        V(lambda e: e.tensor_scalar(out=t2[:], in0=ER[:, 1, :], scalar1=-1.0, scalar2=None, op0=ALU.add), ["ER1"], ["t2"])
        V(lambda e: e.tensor_tensor(out=t3[:], in0=t2[:], in1=lamre[:], op=ALU.mult), ["t2", "lamre"], ["t3"])
        V(lambda e: e.tensor_tensor(out=t4[:], in0=EI[:, 1, :], in1=lamim[:], op=ALU.mult), ["EI1", "lamim"], ["t4"])
        V(lambda e: e.tensor_tensor(out=t3[:], in0=t3[:], in1=t4[:], op=ALU.add), ["t3", "t4"], ["t3"])
        V(lambda e: e.tensor_tensor(out=fre[:], in0=t3[:], in1=t1[:], op=ALU.mult), ["t3", "t1"], ["fre"])
        V(lambda e: e.tensor_tensor(out=t3[:], in0=EI[:, 1, :], in1=lamre[:], op=ALU.mult), ["EI1", "lamre"], ["t3"])
        V(lambda e: e.tensor_tensor(out=t4[:], in0=t2[:], in1=lamim[:], op=ALU.mult), ["t2", "lamim"], ["t4"])
        V(lambda e: e.tensor_tensor(out=t3[:], in0=t3[:], in1=t4[:], op=ALU.subtract), ["t3", "t4"], ["t3"])
        V(lambda e: e.tensor_tensor(out=t3[:], in0=t3[:], in1=t1[:], op=ALU.mult), ["t3", "t1"], ["t3"])
        V(lambda e: e.tensor_scalar(out=fs[:], in0=t3[:], scalar1=sgna[:, 0:1], scalar2=None, op0=ALU.mult), ["t3", "sgna"], ["fs"])
        V(lambda e: e.tensor_scalar(out=nfs[:], in0=fs[:], scalar1=-1.0, scalar2=None, op0=ALU.mult), ["fs"], ["nfs"])
        V(lambda e: e.tensor_tensor(out=u1[:], in0=bst[:], in1=bc(fre[:], 16), op=ALU.mult), ["bst", "fre"], ["u1"])
        V(lambda e: e.tensor_tensor(out=u2[:], in0=bsw[:], in1=bc(fs[:], 16), op=ALU.mult), ["bsw", "fs"], ["u2"])
        V(lambda e: e.tensor_tensor(out=bbst[:], in0=u1[:], in1=u2[:], op=ALU.add), ["u1", "u2"], ["bbst"])
        V(lambda e: e.tensor_tensor(out=u1[:], in0=bsw[:], in1=bc(fre[:], 16), op=ALU.mult), ["bsw", "fre"], ["u1"])
        V(lambda e: e.tensor_tensor(out=u2[:], in0=bst[:], in1=bc(nfs[:], 16), op=ALU.mult), ["bst", "nfs"], ["u2"])
        V(lambda e: e.tensor_tensor(out=bbsw[:], in0=u1[:], in1=u2[:], op=ALU.add), ["u1", "u2"], ["bbsw"])
        LB4 = LB[:].rearrange("p g (s h) -> p g s h", h=16)
        CL4 = CL[:].rearrange("p g (s h) -> p g s h", h=16)
        G(lambda e: e.memset(CL[:], 0.0), [], ["CL"])
        G(lambda e: e.memset(bpad[0][:], 0.0), [], ["bpad0"])
        G(lambda e: e.memset(bpad[1][:], 0.0), [], ["bpad1"])
        for j in range(8):
            V(lambda e, j=j: e.tensor_scalar(out=t1[:], in0=EI[:, j, :], scalar1=sgna[:, 0:1], scalar2=None, op0=ALU.mult), ["EI%d" % j, "sgna"], ["t1"])
            V(lambda e, j=j: e.tensor_tensor(out=u1[:], in0=bbst[:], in1=bc(ER[:, j, :], 16), op=ALU.mult), ["bbst", "ER%d" % j], ["u1"])
            V(lambda e: e.tensor_tensor(out=u2[:], in0=bbsw[:], in1=bc(t1[:], 16), op=ALU.mult), ["bbsw", "t1"], ["u2"])
            V(lambda e, j=j: e.tensor_tensor(out=LB4[:, :, 7 - j, :], in0=u1[:], in1=u2[:], op=ALU.add), ["u1", "u2"], ["LB"])
        for j in range(9):
            V(lambda e, j=j: e.tensor_scalar(out=t1[:], in0=ER[:, j, :], scalar1=nsgna[:, 0:1], scalar2=None, op0=ALU.mult), ["ER%d" % j, "nsgna"], ["t1"])
            V(lambda e, j=j: e.tensor_scalar(out=t2[:], in0=EI[:, j, :], scalar1=-1.0, scalar2=None, op0=ALU.mult), ["EI%d" % j], ["t2"])
            V(lambda e: e.tensor_tensor(out=u1[:], in0=cst[:], in1=bc(t1[:], 16), op=ALU.mult), ["cst", "t1"], ["u1"])
            V(lambda e: e.tensor_tensor(out=u2[:], in0=csw[:], in1=bc(t2[:], 16), op=ALU.mult), ["csw", "t2"], ["u2"])
            V(lambda e, j=j: e.tensor_tensor(out=CL4[:, :, 7 + j, :], in0=u1[:], in1=u2[:], op=ALU.add), ["u1", "u2"], ["CL"])
        if _STOP == 3:
            S.barrier(); return
        for g in range(64):
            k = g % 2
            bp = bpad[k]
            base = bp[:]
            dst = bass.AP(base.tensor, base.offset, [list(base.ap[0]), [128 + 16, 8], [1, 16]])
            G(lambda e, g=g, dst=dst: e.tensor_copy(out=dst, in_=bcmid(bbst[:, g, :], 8)), ["bbst"], ["bpad%d" % k])
            pw = psW[k]
            for s8 in range(8):
                S.emit("pe", lambda e, g=g, s8=s8, bp=bp, pw=pw: e.matmul(
                    pw[:, 0:128], lhsT=bp[:, s8, :], rhs=CL[:, g, (7 - s8) * 16:(7 - s8) * 16 + 128],
                    start=(s8 == 0), stop=(s8 == 7)),
                    reads=[K("bpad%d" % k), K("CL")], writes=[K("psW%d" % k)], acc=(s8 > 0))
            S.emit("pe", lambda e, g=g, pw=pw: e.matmul(pw[:, 128:256], lhsT=LB[:, g, :], rhs=identf[:], start=True, stop=True),
                   reads=[K("LB"), K("identf")], writes=[K("psW%d" % k)])
            S.emit("pe", lambda e, g=g, pw=pw: e.matmul(pw[:, 256:384], lhsT=LB[:, g, :], rhs=pswap[:], start=True, stop=True),
                   reads=[K("LB"), K("pswap")], writes=[K("psW%d" % k)], acc=True)
            w = w5[k]
            if add_d:
                V(lambda e, g=g, w=w, pw=pw: e.scalar_tensor_tensor(out=w[:, 0, :], in0=identf[:], scalar=d8[:, g:g + 1], in1=pw[:, 0:128],
                                                               op0=ALU.mult, op1=ALU.add),
                  ["identf", "d8", "psW%d" % k], ["w5_%d" % k])
            else:
                V(lambda e, w=w, pw=pw: e.tensor_copy(out=w[:, 0, :], in_=pw[:, 0:128]), ["psW%d" % k], ["w5_%d" % k])
            A(lambda e, w=w, pw=pw: e.copy(out=w[:, 1:3, :], in_=pw[:, 128:384].rearrange("p (a b) -> p a b", b=128)), ["psW%d" % k], ["w5_%d" % k])
            G(lambda e, g=g, w=w: e.tensor_copy(out=w[:, 3, :], in_=CL[:, g, 128:256]), ["CL"], ["w5_%d" % k])
            S.emit("dsp", lambda e, g=g, w=w: e.dma_start(out=S5W[g].rearrange("w r c -> r w c"), in_=w[:]),
                   reads=[K("w5_%d" % k)])
        if _STOP == 4:
            S.barrier(); return
        V(lambda e: e.tensor_scalar(out=th8[:], in0=phi[:], scalar1=8.0, scalar2=None, op0=ALU.mult), ["phi"], ["th8"])
        V(lambda e: e.tensor_scalar(out=t2[:], in0=th8[:], scalar1=1.0 / TWO_PI, scalar2=None, op0=ALU.mult), ["th8"], ["t2"])
        V(lambda e: e.tensor_copy(out=ti64[:], in_=t2[:]), ["t2"], ["rti"])
        V(lambda e: e.tensor_copy(out=t2[:], in_=ti64[:]), ["rti"], ["t2"])
        V(lambda e: e.scalar_tensor_tensor(out=th8[:], in0=t2[:], scalar=-TWO_PI, in1=th8[:], op0=ALU.mult, op1=ALU.add), ["t2", "th8"], ["th8"])
        if _STOP == 5:
            S.barrier(); return
        T0f = T0[:].rearrange("p g b -> p (g b)")
        T1f = T1[:].rearrange("p g b -> p (g b)")
        for ti, off in ((0, 0.5 * PI), (1, 0.0)):
            V(lambda e: e.tensor_tensor(out=T1[:], in0=bc(th8[:], 64), in1=bcmid(ramp[:], 64), op=ALU.mult), ["th8", "ramp"], ["T1"])
            if _STOP == 6:
                S.barrier(); return
            sin_reduced(T1, T1[:], 1.0, off, T0, tiT, "T1", ["T1"])
            if _STOP == 7:
                S.barrier(); return
            if ti == 1:
                V(lambda e: e.tensor_scalar(out=T1[:], in0=T1[:], scalar1=nsgna[:, 0:1], scalar2=None, op0=ALU.mult), ["T1", "nsgna"], ["T1"])
            for b8 in range(8):
                S.emit("dsp", lambda e, b8=b8, ti=ti: e.dma_start(out=S5T[b8, ti], in_=T1f[:, b8 * 512:(b8 + 1) * 512]), reads=[K("T1")])
        if _STOP == 8:
            S.barrier(); return
        A(lambda e: e.activation(out=t1[:], in_=aa[:], func=AF.Exp, scale=8.0), ["aa"], ["t1"])
        V(lambda e: e.tensor_copy(out=T0[:], in_=bc(t1[:], 64)), ["t1", "rtf"], ["T0", "rtf"])
        V(lambda e: e.tensor_scalar(out=T0[:, :, 0:1], in0=T0[:, :, 0:1], scalar1=0.0, scalar2=None, op0=ALU.mult), ["T0"], ["T0"])
        for b8 in range(8):
            S.emit("dsp", lambda e, b8=b8: e.dma_start(out=S5T[b8, 2], in_=T0f[:, b8 * 512:(b8 + 1) * 512]), reads=[K("T0")])
        V(lambda e: e.tensor_copy(out=r8_out, in_=t1[:]), ["t1"], ["r8" + X])
        S.barrier()


def build_program(NT, passes=("P", "A", "B"), n_cores=8):
    nc = bass.Bass("TRN2", target_bir_lowering=False)
    sh = input_shapes(NT)
    dr = {n: nc.dram_tensor(n, s, F32, kind="ExternalInput").ap() for n, s in sh.items()}
    yB = nc.dram_tensor("yB", [NT * TT, D], F32, kind="ExternalOutput").ap()
    Y1 = nc.dram_tensor("Y1s", [NT, 2048, TT], F32, kind="Internal").ap()
    S5W = {X: nc.dram_tensor("S5W" + X, [64, 4, 128, 128], BF16, kind="Internal").ap() for X in "AB"}
    S5T = {X: nc.dram_tensor("S5T" + X, [8, 3, 128, 512], F32, kind="Internal").ap() for X in "AB"}
    WBs = {"in": nc.dram_tensor("WB_in", [82, 128, 2048], BF16, kind="Internal").ap(),
           "glu": nc.dram_tensor("WB_glu", [8, 128, 1024], BF16, kind="Internal").ap(),
           "s5o": nc.dram_tensor("WB_s5o", [16, 128, 1024], BF16, kind="Internal").ap(),
           "rwo": nc.dram_tensor("WB_rwo", [16, 128, 1024], BF16, kind="Internal").ap(),
           "wo": nc.dram_tensor("WB_wo", [16, 128, 2048], BF16, kind="Internal").ap()}
    S = Sched(nc)
    S.open()
    with ExitStack() as es:
        cnt = [0]

        def sb(shape, dt=F32):
            cnt[0] += 1
            return es.enter_context(nc.sbuf_tensor("m_%d" % cnt[0], shape, dt))

        def ps(shape, dt=F32):
            cnt[0] += 1
            return es.enter_context(nc.psum_tensor("mp_%d" % cnt[0], shape, dt))

        r8 = {X: sb([128, 64]) for X in "AB"}
        gen_s5(nc, S, dr, "A", False, S5W["A"], S5T["A"], r8["A"][:])
        gen_s5(nc, S, dr, "B", True, S5W["B"], S5T["B"], r8["B"][:])

        V = lambda fn, r, w, **kw: S.emit("dve", fn, reads=r, writes=w, **kw)
        A = lambda fn, r, w, **kw: S.emit("act", fn, reads=r, writes=w, **kw)
        G = lambda fn, r, w, **kw: S.emit("pool", fn, reads=r, writes=w, **kw)
        P = lambda fn, r, w, **kw: S.emit("pe", fn, reads=r, writes=w, **kw)
        LD = lambda fn, r, w: S.emit("dsp", fn, reads=r, writes=w)
        LC = lambda fn, r, w: S.emit("dpool", fn, reads=r, writes=w)

        identf = sb([128, 128]); identb = sb([128, 128], BF16); bones = sb([128, 128])
        selb = sb([128, 8, 240], BF16); selt = sb([128, 8, 240]); mask_g = sb([64, 128]); mask_lo = sb([64, 64])
        rst = sb([128, TT]); cm = sb([128, 3 * NT])
        vec = {}
        for n in ("pre_g", "mu", "k_k", "k_a", "r_k", "ln_w", "ln_b", "w0A", "w0B", "a0A", "a0B"):
            vec[n] = sb(sh[n])
            LD(lambda e, n=n: e.dma_start(out=vec[n][:], in_=dr[n]), [], ["c_" + n])
        omm = sb([128, 26]); hmu = sb([128, 26]); omka = sb([128, 8])
        wup = sb([128, 1024]); aup = sb([128, 1024])
        for t, n in ((identf, "identf"), (bones, "bones"), (selt, "seltpad"), (mask_g, "mask_g"), (mask_lo, "mask_lo"),
                     (rst, "rst"), (cm, "cm")):
            LD(lambda e, t=t, n=n: e.dma_start(out=t[:], in_=dr[n]), [], ["c_" + n])
        LC(lambda e: e.dma_start(out=selb[:], in_=dr["selpad"]), [], ["c_selb"])
        LD(lambda e: e.dma_start(out=wup[0:64, :], in_=dr["wupA"]), [], ["c_wup"])
        LD(lambda e: e.dma_start(out=wup[64:128, :], in_=dr["wupB"]), [], ["c_wup"])
        LD(lambda e: e.dma_start(out=aup[0:64, :], in_=dr["aupA"]), [], ["c_aup"])
        LD(lambda e: e.dma_start(out=aup[64:128, :], in_=dr["aupB"]), [], ["c_aup"])
        V(lambda e: e.tensor_copy(out=identb[:], in_=identf[:]), ["c_identf"], ["c_identb"])
        V(lambda e: e.tensor_scalar(out=omm[:], in0=vec["mu"][:], scalar1=-1.0, scalar2=1.0, op0=ALU.mult, op1=ALU.add), ["c_mu"], ["c_omm"])
        V(lambda e: e.tensor_scalar(out=hmu[:], in0=vec["mu"][:], scalar1=0.5, scalar2=None, op0=ALU.mult), ["c_mu"], ["c_hmu"])
        V(lambda e: e.tensor_scalar(out=omka[:], in0=vec["k_a"][:], scalar1=-1.0, scalar2=1.0, op0=ALU.mult, op1=ALU.add), ["c_k_a"], ["c_omka"])

        NS = 18
        arena = sb([128, NS, 516])
        W = lambda i: arena[:, i, 0:512]
        WX = lambda i: arena[:, i, 0:514]
        wk = lambda *ids: ["w%d" % i for i in ids]
        hnT = sb([128, 16, TT + 2], BF16)
        xs = sb([128, D]); xn = sb([128, D], BF16); stat = sb([128, 8])
        wbuf = [sb([128, 16, 128], BF16), sb([128, 16, 128], BF16)]
        s5w = sb([128, 8, 4, 128], BF16); s5t = sb([128, 3, 512])
        ub = sb([128, 512], BF16); U8 = sb([128, 8, 64], BF16); Xp = sb([128, 8, 64], BF16)
        seltb = sb([128, 8, 240], BF16); y8h = sb([128, 8, 64], BF16); y8l = sb([128, 8, 64], BF16)
        V(lambda e: e.tensor_copy(out=seltb[:], in_=selt[:]), ["c_seltpad"], ["c_seltb"])
        X1c = sb([128, 64]); X2c = sb([128, 64]); tmp8 = sb([128, 8]); tmp8b = sb([128, 8])
        LH = sb([128, 8, 128], BF16); RH = sb([128, 8, 128], BF16)
        Btok = sb([64, 8, 128], BF16); Ktok = sb([64, 8, 128], BF16); Vtok = sb([64, 8, 128], BF16)
        STb = sb([128, 8, 64], BF16); Vb = sb([128, 512], BF16)
        NbA = sb([64, 16, 64], BF16); Nt = sb([64, 16, 64], BF16); Nka = sb([64, 16, 64], BF16); Mbr = sb([64, 16, 64], BF16); Mkr = sb([64, 16, 64], BF16)
        Wm = sb([64, 16, 64], BF16); QT = sb([64, 128], BF16); QTf = sb([64, 128]); PT = sb([64, 128], BF16); YT = sb([64, 8, 128])
        ST = sb([128, 8, 64])
        zb = sb([128, 8, 512], BF16); zzb = sb([128, 8, 512], BF16); yrwb = sb([128, 8, 512], BF16); hb = sb([128, 16, 512], BF16)
        gsl = s5t[:, 0, :]
        pA = ps([128, 512]); pB = ps([128, 512]); pC = ps([128, 512]); pD = ps([128, 512])
        pP = [ps([128, 512]), ps([128, 512])]
        pT = ps([128, 1024], BF16)
        pH = ps([128, 16])
        PK = {id(pA): "pA", id(pB): "pB", id(pC): "pC", id(pD): "pD", id(pP[0]): "pP0", id(pP[1]): "pP1"}
        kk_ = lambda t: PK[id(t)]

        G(lambda e: e.memset(ST[:], 0.0), [], ["ST"])
        G(lambda e: e.memset(X1c[:], 0.0), [], ["X1c"])
        G(lambda e: e.memset(X2c[:], 0.0), [], ["X2c"])

        wstate = {"i": 0}
        w_in_v = dr["w_in"].rearrange("(kc p) m -> p kc m", p=128)

        def load_w(src_view, nk, width):
            i = wstate["i"]
            wstate["i"] = 1 - i
            wb = wbuf[i]
            dst = wb[:].rearrange("p a b -> p (a b)")[:, 0:nk * width].rearrange("p (a b) -> p a b", b=width)
            LC(lambda e: e.dma_start(out=dst, in_=src_view), [], ["wbuf%d" % i])
            return dst, "wbuf%d" % i

        def load_wb(name, blk, nk, width):
            i = wstate["i"]
            wstate["i"] = 1 - i
            wb = wbuf[i]
            flat = wb[:].rearrange("p a b -> p (a b)")[:, 0:nk * width]
            LD(lambda e: e.dma_start(out=flat, in_=WBs[name][blk]), ["WB_%s_%d" % (name, blk)], ["wbuf%d" % i])
            return flat.rearrange("p (a b) -> p a b", b=width), "wbuf%d" % i

        def precast(name, blk, src_view, nk, width):
            dst, key = load_w(src_view, nk, width)
            LD(lambda e: e.dma_start(out=WBs[name][blk], in_=dst.rearrange("p a b -> p (a b)")), [key], ["WB_%s_%d" % (name, blk)])

        pstate = {"i": 0}

        def proj(cb, halo):
            wv, wkey = load_wb("in", cb, 16, 128)
            i = pstate["i"]
            pstate["i"] = 1 - i
            pt = pP[i]
            for kc in range(16):
                P(lambda e, kc=kc: e.matmul(pt[:], lhsT=wv[:, kc, :], rhs=hnT[:, kc, 1:TT + 1], start=(kc == 0), stop=(kc == 15)),
                  [wkey, "hnT"], ["pP%d" % i], acc=(kc > 0))
            if halo:
                for kc in range(16):
                    P(lambda e, kc=kc: e.matmul(pH[:, 0:2], lhsT=wv[:, kc, :], rhs=hnT[:, kc, 0:TT + 2:TT + 1], start=(kc == 0), stop=(kc == 15)),
                      [wkey, "hnT"], ["pH"], acc=(kc > 0))
            return pt, "pP%d" % i

        def mixed(cb, mi, dst, dkey, t0, t1):
            pt, pk = proj(cb, True)
            A(lambda e: e.copy(out=arena[:, t0, 1:513], in_=pt[:]), [pk], wk(t0))
            V(lambda e: e.tensor_copy(out=arena[:, t0, 0:514:513], in_=pH[:, 0:2]), ["pH"], wk(t0))
            V(lambda e: e.tensor_tensor(out=W(t1), in0=arena[:, t0, 0:512], in1=arena[:, t0, 2:514], op=ALU.add), wk(t0), wk(t1))
            V(lambda e: e.tensor_scalar(out=arena[:, t0, 1:513], in0=arena[:, t0, 1:513], scalar1=omm[:, mi:mi + 1], scalar2=None, op0=ALU.mult),
              wk(t0) + ["c_omm"], wk(t0))
            V(lambda e: e.scalar_tensor_tensor(out=dst, in0=W(t1), scalar=hmu[:, mi:mi + 1], in1=arena[:, t0, 1:513], op0=ALU.mult, op1=ALU.add),
              wk(t0, t1) + ["c_hmu"], dkey)

        def stage_load(xsrc, j):
            for q in range(-1, 4):
                if q < 0:
                    npart = 2
                    LD(lambda e: e.dma_start(out=xs[0:2, :], in_=xsrc[j, 0:TT + 2:TT + 1, :]), [], ["xs"])
                else:
                    npart = 128
                    LD(lambda e, q=q: e.dma_start(out=xs[:, :], in_=xsrc[j, 1 + 128 * q:129 + 128 * q, :]), [], ["xs"])
                sl = slice(0, npart)
                V(lambda e: e.memset(stat[:], 0.0), [], ["stat"])
                A(lambda e, sl=sl: e.activation(out=xn[sl, :], in_=xs[sl, :], func=AF.Square, accum_out=stat[sl, 0:1]), ["xs", "stat"], ["xn", "stat"])
                V(lambda e, sl=sl: e.tensor_scalar(out=stat[sl, 1:2], in0=stat[sl, 0:1], scalar1=1.0 / D, scalar2=1e-6, op0=ALU.mult, op1=ALU.add), ["stat"], ["stat"])
                A(lambda e, sl=sl: e.activation(out=stat[sl, 1:2], in_=stat[sl, 1:2], func=AF.Sqrt), ["stat"], ["stat"])
                V(lambda e, sl=sl: e.reciprocal(out=stat[sl, 2:3], in_=stat[sl, 1:2]), ["stat"], ["stat"])
                A(lambda e, sl=sl: e.activation(out=xn[sl, :], in_=xs[sl, :], func=AF.Copy, scale=stat[sl, 2:3]), ["xs", "stat"], ["xn"])
                if q < 0:
                    for kc in range(16):
                        P(lambda e, kc=kc: e.transpose(out=pT[:, kc * 2:kc * 2 + 2], in_=xn[0:2, kc * 128:(kc + 1) * 128], identity=identb[0:2, 0:2]),
                          ["xn", "c_identb"], ["pT"], acc=(kc > 0))
                    V(lambda e: e.tensor_tensor(out=hnT[:, :, 0:TT + 2:TT + 1], in0=pT[:, 0:32].rearrange("p (a b) -> p a b", b=2),
                                                in1=bc(vec["pre_g"][:], 2), op=ALU.mult), ["pT", "c_pre_g"], ["hnT"])
                else:
                    for k4 in range(4):
                        for kk2 in range(4):
                            kc = k4 * 4 + kk2
                            P(lambda e, kc=kc, kk2=kk2: e.transpose(out=pT[:, kk2 * 128:(kk2 + 1) * 128], in_=xn[:, kc * 128:(kc + 1) * 128], identity=identb[:]),
                              ["xn", "c_identb"], ["pT"], acc=(kk2 > 0))
                        V(lambda e, k4=k4, q=q: e.tensor_tensor(out=hnT[:, k4 * 4:k4 * 4 + 4, 1 + 128 * q:129 + 128 * q],
                                                              in0=pT[:, 0:512].rearrange("p (a b) -> p a b", b=128),
                                                              in1=bc(vec["pre_g"][:, k4 * 4:k4 * 4 + 4], 128), op=ALU.mult), ["pT", "c_pre_g"], ["hnT"])

        def stage_s5(X, j, cb, last_pass, store=True):
            pt, pk = proj(cb, False)
            A(lambda e: e.copy(out=ub[:], in_=pt[:]), [pk], ["ub"])
            LD(lambda e: e.dma_start(out=s5w[:], in_=S5W[X][cb * 8:(cb + 1) * 8].rearrange("g w r c -> r g w c")), [], ["s5w"])
            LD(lambda e: e.dma_start(out=s5t[:], in_=S5T[X][cb].rearrange("t p f -> p t f")), [], ["s5t"])
            for g8 in range(8):
                for s8 in range(8):
                    P(lambda e, g8=g8, s8=s8: e.matmul(pA[:, g8 * 64:(g8 + 1) * 64], lhsT=selb[:, g8, 112 - 16 * s8:240 - 16 * s8],
                                                        rhs=ub[:, s8:512:8], start=(s8 == 0), stop=(s8 == 7)),
                      ["c_selb", "ub"], ["pA"], acc=not (g8 == 0 and s8 == 0))
            V(lambda e: e.tensor_copy(out=U8[:], in_=pA[:].rearrange("p (a b) -> p a b", b=64)), ["pA"], ["U8"])
            for g8 in range(8):
                P(lambda e, g8=g8: e.matmul(pC[:, g8 * 64:(g8 + 1) * 64], lhsT=s5w[:, g8, 1, :], rhs=U8[:, g8, :], start=True, stop=True),
                  ["s5w", "U8"], ["pC"], acc=(g8 > 0))
            for g8 in range(8):
                P(lambda e, g8=g8: e.matmul(pD[:, g8 * 64:(g8 + 1) * 64], lhsT=s5w[:, g8, 2, :], rhs=U8[:, g8, :], start=True, stop=True),
                  ["s5w", "U8"], ["pD"], acc=(g8 > 0))
            Ct, S1t, Rt = s5t[:, 0, :], s5t[:, 1, :], s5t[:, 2, :]
            V(lambda e: e.tensor_tensor(out=W(0), in0=pC[:], in1=Ct, op=ALU.mult), ["pC", "s5t"], wk(0))
            V(lambda e: e.tensor_tensor(out=W(1), in0=pD[:], in1=S1t, op=ALU.mult), ["pD", "s5t"], wk(1))
            G(lambda e: e.tensor_tensor(out=W(2), in0=W(0), in1=W(1), op=ALU.add), wk(0, 1), wk(2))
            V(lambda e: e.tensor_tensor(out=W(0), in0=pD[:], in1=Ct, op=ALU.mult), ["pD", "s5t"], wk(0))
            V(lambda e: e.tensor_tensor(out=W(1), in0=pC[:], in1=S1t, op=ALU.mult), ["pC", "s5t"], wk(1))
            G(lambda e: e.tensor_tensor(out=W(3), in0=W(0), in1=W(1), op=ALU.subtract), wk(0, 1), wk(3))
            gs = slice(cb * 8, cb * 8 + 8)
            V(lambda e: e.tensor_tensor(out=tmp8[:], in0=r8[X][:, gs], in1=X1c[:, gs], op=ALU.mult), ["X1c"], ["tmp8"])
            V(lambda e: e.tensor_tensor(out=arena[:, 2, 0:512:64], in0=arena[:, 2, 0:512:64], in1=tmp8[:], op=ALU.add), wk(2) + ["tmp8"], wk(2))
            V(lambda e: e.tensor_tensor(out=tmp8[:], in0=r8[X][:, gs], in1=X2c[:, gs], op=ALU.mult), ["X2c"], ["tmp8"])
            V(lambda e: e.tensor_tensor(out=arena[:, 3, 0:512:64], in0=arena[:, 3, 0:512:64], in1=tmp8[:], op=ALU.add), wk(3) + ["tmp8"], wk(3))
            V(lambda e: e.tensor_tensor_scan(out=W(4), data0=Rt, data1=W(2), initial=0.0, op0=ALU.mult, op1=ALU.add), wk(2) + ["s5t"], wk(4))
            V(lambda e: e.tensor_tensor_scan(out=W(5), data0=Rt, data1=W(3), initial=0.0, op0=ALU.mult, op1=ALU.add), wk(3) + ["s5t"], wk(5))
            V(lambda e: e.tensor_tensor(out=W(0), in0=W(4), in1=Ct, op=ALU.mult), wk(4) + ["s5t"], wk(0))
            V(lambda e: e.tensor_tensor(out=W(1), in0=W(5), in1=S1t, op=ALU.mult), wk(5) + ["s5t"], wk(1))
            G(lambda e: e.tensor_tensor(out=W(2), in0=W(0), in1=W(1), op=ALU.subtract), wk(0, 1), wk(2))
            x1v = W(2).rearrange("p (a b) -> p a b", b=64)
            G(lambda e: e.tensor_copy(out=Xp[:, :, 1:64], in_=x1v[:, :, 0:63]), wk(2), ["Xp"])
            G(lambda e: e.tensor_copy(out=Xp[:, :, 0:1], in_=X1c[:, gs].unsqueeze(2)), ["X1c"], ["Xp"])
            V(lambda e: e.tensor_copy(out=X1c[:, gs], in_=arena[:, 2, 63:512:64]), wk(2) + ["Xp"], ["X1c"])
            V(lambda e: e.tensor_tensor(out=tmp8[:], in0=arena[:, 5, 63:512:64], in1=s5t[:, 0, 63:512:64], op=ALU.mult), wk(5) + ["s5t"], ["tmp8"])
            V(lambda e: e.tensor_tensor(out=tmp8b[:], in0=arena[:, 4, 63:512:64], in1=s5t[:, 1, 63:512:64], op=ALU.mult), wk(4) + ["s5t"], ["tmp8b"])
            V(lambda e: e.tensor_tensor(out=X2c[:, gs], in0=tmp8[:], in1=tmp8b[:], op=ALU.add), ["tmp8", "tmp8b"], ["X2c"])
            if not last_pass and not store:
                return
            for g8 in range(8):
                P(lambda e, g8=g8: e.matmul(pB[:, g8 * 64:(g8 + 1) * 64], lhsT=s5w[:, g8, 0, :], rhs=U8[:, g8, :], start=True, stop=False),
                  ["s5w", "U8"], ["pB"], acc=(g8 > 0))
                P(lambda e, g8=g8: e.matmul(pB[:, g8 * 64:(g8 + 1) * 64], lhsT=s5w[:, g8, 3, :], rhs=Xp[:, g8, :], start=False, stop=True),
                  ["s5w", "Xp"], ["pB"], acc=True)
            A(lambda e: e.copy(out=W(3), in_=pB[:]), ["pB"], wk(3))
            y8 = W(3).rearrange("p (a b) -> p a b", b=64)
            V(lambda e: e.tensor_copy(out=y8h[:], in_=y8), wk(3), ["y8h"])
            V(lambda e: e.tensor_tensor(out=y8, in0=y8, in1=y8h[:], op=ALU.subtract), wk(3) + ["y8h"], wk(3))
            G(lambda e: e.tensor_copy(out=y8l[:], in_=y8), wk(3), ["y8l"])
            for t8 in range(8):
                for g8 in range(8):
                    P(lambda e, g8=g8, t8=t8: e.matmul(pA[:, t8 * 64:(t8 + 1) * 64], lhsT=seltb[:, t8, 112 - 16 * g8:240 - 16 * g8],
                                                        rhs=y8h[:, g8, :], start=(g8 == 0), stop=False),
                      ["c_seltb", "y8h"], ["pA"], acc=not (g8 == 0 and t8 == 0))
                    P(lambda e, g8=g8, t8=t8: e.matmul(pA[:, t8 * 64:(t8 + 1) * 64], lhsT=seltb[:, t8, 112 - 16 * g8:240 - 16 * g8],
                                                        rhs=y8l[:, g8, :], start=False, stop=(g8 == 7)),
                      ["c_seltb", "y8l"], ["pA"], acc=True)
            V(lambda e: e.tensor_copy(out=W(6).rearrange("p (b t) -> p t b", t=8), in_=pA[:].rearrange("p (t b) -> p t b", b=64)), ["pA"], wk(6))
            if not last_pass:
                if store:
                    LD(lambda e: e.dma_start(out=Y1[j, cb * 128:(cb + 1) * 128, :], in_=W(6)), wk(6), ["Y1_%d" % j])
            else:
                jm = NT - 1 - j
                LD(lambda e: e.dma_start(out=W(7), in_=Y1[jm, cb * 128:(cb + 1) * 128, :]), ["Y1_%d" % jm], wk(7))
                rev = arena[:, 7, 511::-1] if False else bass.AP(arena[:].tensor, arena[:, 7, 511:512].offset, [list(arena[:].ap[0]), [-1, 512]])
                V(lambda e: e.tensor_tensor(out=W(0), in0=W(6), in1=rev, op=ALU.add), wk(6, 7), wk(0))
                A(lambda e: e.activation(out=W(1), in_=W(0), func=AF.Square), wk(0), wk(1))
                V(lambda e: e.tensor_scalar(out=W(1), in0=W(1), scalar1=0.044715, scalar2=1.0, op0=ALU.mult, op1=ALU.add), wk(1), wk(1))
                V(lambda e: e.tensor_tensor(out=W(1), in0=W(1), in1=W(0), op=ALU.mult), wk(0, 1), wk(1))
                A(lambda e: e.activation(out=W(1), in_=W(1), func=AF.Sigmoid, scale=1.5957691216057308), wk(1), wk(1))
                V(lambda e: e.tensor_tensor(out=zb[:, cb, :], in0=W(0), in1=W(1), op=ALU.mult), wk(0, 1), ["zb"])

        LORA0, LORA1 = 16, 17

        def stage_lora(X):
            mixed(40, 24, W(LORA0), wk(LORA0), 0, 1)
            A(lambda e: e.activation(out=W(LORA0), in_=W(LORA0), func=AF.Tanh), wk(LORA0), wk(LORA0))
            mixed(41, 25, W(LORA1), wk(LORA1), 0, 1)

        def stage_rwkv(X, j, hp, last_pass, store=True):
            lo = 0 if X == "A" else 64
            R_, K_, V_, LD_, AA_, KK_, BON_, BE_, KD_, CUM_, EN_, EP_, EV_ = 3, 4, 5, 6, 7, 8, 9, 10, 11, 12, 13, 14, 15
            mixed(16 + hp, hp, W(R_), wk(R_), 0, 1)
            mixed(24 + hp, 8 + hp, W(K_), wk(K_), 0, 1)
            mixed(32 + hp, 16 + hp, W(V_), wk(V_), 0, 1)
            if X == _os.environ.get('MPASS', 'A') and int(_os.environ.get('RSTOP', '0')) == 1:
                return
            hs = slice(hp * 128, (hp + 1) * 128)
            col = lambda t: t[:, hp:hp + 1]
            P(lambda e: e.matmul(pA[:], lhsT=wup[lo:lo + 64, hs], rhs=arena[lo:lo + 64, LORA0, 0:512], start=True, stop=True), ["c_wup"] + wk(LORA0), ["pA"], rg=lo)
            A(lambda e: e.activation(out=W(LD_), in_=pA[:], func=AF.Sigmoid, bias=col(vec["w0" + X])), ["pA", "c_w0" + X], wk(LD_))
            V(lambda e: e.tensor_scalar(out=W(LD_), in0=W(LD_), scalar1=-math.exp(-0.5), scalar2=None, op0=ALU.mult), wk(LD_), wk(LD_))
            P(lambda e: e.matmul(pB[:], lhsT=aup[lo:lo + 64, hs], rhs=arena[lo:lo + 64, LORA1, 0:512], start=True, stop=True), ["c_aup"] + wk(LORA1), ["pB"], rg=lo)
            A(lambda e: e.activation(out=W(AA_), in_=pB[:], func=AF.Sigmoid, bias=col(vec["a0" + X])), ["pB", "c_a0" + X], wk(AA_))
            V(lambda e: e.tensor_scalar(out=W(KK_), in0=W(K_), scalar1=col(vec["k_k"]), scalar2=None, op0=ALU.mult), wk(K_) + ["c_k_k"], wk(KK_))
            A(lambda e: e.activation(out=W(0), in_=W(KK_), func=AF.Square), wk(KK_), wk(0))
            P(lambda e: e.matmul(pC[:], lhsT=bones[:], rhs=W(0), start=True, stop=True), ["c_bones"] + wk(0), ["pC"])
            V(lambda e: e.tensor_scalar(out=W(1), in0=pC[:], scalar1=1e-12, scalar2=None, op0=ALU.add), ["pC"], wk(1))
            A(lambda e: e.activation(out=W(1), in_=W(1), func=AF.Sqrt), wk(1), wk(1))
            V(lambda e: e.reciprocal(out=W(1), in_=W(1)), wk(1), wk(1))
            V(lambda e: e.tensor_tensor(out=W(KK_), in0=W(KK_), in1=W(1), op=ALU.mult), wk(KK_, 1), wk(KK_))
            if last_pass:
                V(lambda e: e.scalar_tensor_tensor(out=W(0), in0=W(R_), scalar=col(vec["r_k"]), in1=W(K_), op0=ALU.mult, op1=ALU.mult), wk(R_, K_) + ["c_r_k"], wk(0))
                P(lambda e: e.matmul(pD[:], lhsT=bones[:], rhs=W(0), start=True, stop=True), ["c_bones"] + wk(0), ["pD"])
                V(lambda e: e.tensor_tensor(out=W(BON_), in0=pD[:], in1=W(V_), op=ALU.mult), ["pD"] + wk(V_), wk(BON_))
            V(lambda e: e.tensor_tensor(out=W(BE_), in0=W(KK_), in1=W(AA_), op=ALU.mult), wk(KK_, AA_), wk(BE_))
            V(lambda e: e.tensor_scalar(out=W(1), in0=W(AA_), scalar1=col(vec["k_a"]), scalar2=col(omka), op0=ALU.mult, op1=ALU.add), wk(AA_) + ["c_k_a", "c_omka"], wk(1))
            V(lambda e: e.tensor_tensor(out=W(KD_), in0=W(K_), in1=W(1), op=ALU.mult), wk(K_, 1), wk(KD_))
            V(lambda e: e.tensor_tensor_scan(out=W(CUM_), data0=rst[:], data1=W(LD_), initial=0.0, op0=ALU.mult, op1=ALU.add), wk(LD_) + ["c_rst"], wk(CUM_))
            A(lambda e: e.activation(out=W(EN_), in_=W(CUM_), func=AF.Exp, scale=-1.0), wk(CUM_), wk(EN_))
            A(lambda e: e.activation(out=W(EP_), in_=W(CUM_), func=AF.Exp), wk(CUM_), wk(EP_))
            V(lambda e: e.tensor_tensor(out=W(1), in0=W(CUM_), in1=W(LD_), op=ALU.subtract), wk(CUM_, LD_), wk(1))
            A(lambda e: e.activation(out=W(EV_), in_=W(1), func=AF.Exp), wk(1), wk(EV_))
            if X == _os.environ.get('MPASS', 'A') and int(_os.environ.get('RSTOP', '0')) == 2:
                return
            c3 = lambda i: W(i).rearrange("p (c t) -> p c t", t=64)
            V(lambda e: e.tensor_tensor(out=LH[:, :, 0:64], in0=c3(BE_), in1=c3(EN_), op=ALU.mult), wk(BE_, EN_), ["LH"])
            G(lambda e: e.tensor_tensor(out=LH[:, :, 64:128], in0=c3(KD_), in1=c3(EN_), op=ALU.mult), wk(KD_, EN_), ["LH"])
            V(lambda e: e.scalar_tensor_tensor(out=RH[:, :, 0:64], in0=c3(KK_), scalar=-1.0, in1=c3(EV_), op0=ALU.mult, op1=ALU.mult), wk(KK_, EV_), ["RH"])
            G(lambda e: e.tensor_tensor(out=RH[:, :, 64:128], in0=c3(R_), in1=c3(EP_), op=ALU.mult), wk(R_, EP_), ["RH"])
            if X == _os.environ.get('MPASS', 'A') and int(_os.environ.get('RSTOP', '0')) == 3:
                return
            V(lambda e: e.tensor_copy(out=Vb[:], in_=W(V_)), wk(V_), ["Vb"])
            for (src_fn, dstt, dk, rk) in ((lambda c: LH[:, c, 0:64], Btok, "Btok", ["LH"]), (lambda c: LH[:, c, 64:128], Ktok, "Ktok", ["LH"]),
                                           (lambda c: Vb[:, c * 64:(c + 1) * 64], Vtok, "Vtok", ["Vb"])):
                for c in range(8):
                    P(lambda e, c=c, src_fn=src_fn: e.transpose(out=pT[0:64, c * 128:(c + 1) * 128], in_=src_fn(c), identity=identb[:]),
                      rk + ["c_identb"], ["pT"], acc=(c > 0))
                V(lambda e, dstt=dstt: e.tensor_copy(out=dstt[:], in_=pT[0:64, :].rearrange("p (a b) -> p a b", b=128)), ["pT"], [dk])
            if X == _os.environ.get('MPASS', 'A') and int(_os.environ.get('RSTOP', '0')) == 4:
                return
            for h in range(2):
                ph = slice(64 * h, 64 * h + 64)
                for (lcol, pts, outs) in ((slice(0, 64), (pA, pB), (NbA, Mbr)), (slice(64, 128), (pC, pD), (Nka, Mkr))):
                    for half in range(2):
                        pt_ = pts[half]
                        for c4 in range(4):
                            c = half * 4 + c4
                            P(lambda e, c=c, c4=c4, pt_=pt_, lcol=lcol, ph=ph: e.matmul(pt_[0:64, c4 * 128:(c4 + 1) * 128], lhsT=LH[ph, c, lcol], rhs=RH[ph, c, :], start=True, stop=True),
                              ["LH", "RH"], [kk_(pt_)], acc=(c4 > 0), rg=64 * h)
                        pv = pt_[0:64, :].rearrange("p (a b) -> p a b", b=128)
                        hc = slice(h * 8 + half * 4, h * 8 + half * 4 + 4)
                        V(lambda e, pv=pv, hc=hc, o=outs[0]: e.tensor_tensor(out=o[:, hc, :], in0=pv[:, :, 0:64], in1=bcmid(mask_g[:, 0:64], 4), op=ALU.mult),
                          [kk_(pt_), "c_mask_g"], ["G_%d" % id(outs[0])])
                        V(lambda e, pv=pv, hc=hc, o=outs[1]: e.tensor_tensor(out=o[:, hc, :], in0=pv[:, :, 64:128], in1=bcmid(mask_g[:, 64:128], 4), op=ALU.mult),
                          [kk_(pt_), "c_mask_g"], ["G_%d" % id(outs[1])])
                pt_ = pP[h]
                for c in range(8):
                    P(lambda e, c=c, pt_=pt_, ph=ph: e.matmul(pt_[0:64, c * 64:(c + 1) * 64], lhsT=RH[ph, c, 0:64], rhs=LH[ph, c, 0:64], start=True, stop=True),
                      ["LH", "RH"], [kk_(pt_)], acc=(c > 0), rg=64 * h)
                V(lambda e, pt_=pt_, h=h: e.tensor_tensor(out=Nt[:, h * 8:h * 8 + 8, :], in0=pt_[0:64, :].rearrange("p (a b) -> p a b", b=64),
                                                          in1=bcmid(mask_lo[:], 8), op=ALU.mult), [kk_(pt_), "c_mask_lo"], ["G_%d" % id(Nt)])
            if X == _os.environ.get('MPASS', 'A') and int(_os.environ.get('RSTOP', '0')) == 5:
                return
            gk = lambda t: "G_%d" % id(t)
            V(lambda e: e.tensor_tensor(out=Wm[:], in0=NbA[:], in1=bcmid(identf[0:64, 0:64], 16), op=ALU.add), [gk(NbA), "c_identf"], ["Wm"])
            banks = [pA, pB, pC, pD]
            v3 = lambda t: t[0:64, :].rearrange("p (a b) -> p a b", b=64)

            def sq_mm(last):
                for hc in range(16):
                    if not last:
                        pt_ = banks[hc // 8]
                        P(lambda e, hc=hc, pt_=pt_: e.matmul(pt_[0:64, (hc % 8) * 64:(hc % 8 + 1) * 64], lhsT=Nt[:, hc, :], rhs=NbA[:, hc, :], start=True, stop=True),
                          [gk(Nt), gk(NbA)], [kk_(pt_)], acc=(hc % 8 > 0))
                    pt_ = banks[2 + hc // 8]
                    P(lambda e, hc=hc, pt_=pt_: e.matmul(pt_[0:64, (hc % 8) * 64:(hc % 8 + 1) * 64], lhsT=NbA[:, hc, :], rhs=Nt[:, hc, :], start=True, stop=True),
                      [gk(Nt), gk(NbA)], [kk_(pt_)], acc=(hc % 8 > 0))

            def sq_evac(last):
                if not last:
                    A(lambda e: e.copy(out=NbA[:, 0:8, :], in_=v3(pA)), ["pA"], [gk(NbA)])
                    A(lambda e: e.copy(out=NbA[:, 8:16, :], in_=v3(pB)), ["pB"], [gk(NbA)])
                V(lambda e: e.tensor_copy(out=Nt[:, 0:8, :], in_=v3(pC)), ["pC"], [gk(Nt)])
                V(lambda e: e.tensor_copy(out=Nt[:, 8:16, :], in_=v3(pD)), ["pD"], [gk(Nt)])

            def w_mm():
                for hc in range(16):
                    pt_ = pP[hc // 8]
                    osl = pt_[0:64, (hc % 8) * 64:(hc % 8 + 1) * 64]
                    P(lambda e, hc=hc, osl=osl: e.matmul(osl, lhsT=identb[0:64, 0:64], rhs=Wm[:, hc, :], start=True, stop=False), ["c_identb", "Wm"], [kk_(pt_)], acc=(hc % 8 > 0))
                    P(lambda e, hc=hc, osl=osl: e.matmul(osl, lhsT=Nt[:, hc, :], rhs=Wm[:, hc, :], start=False, stop=True), [gk(Nt), "Wm"], [kk_(pt_)], acc=True)

            def w_evac():
                A(lambda e: e.copy(out=Wm[:, 0:8, :], in_=v3(pP[0])), ["pP0"], ["Wm"])
                V(lambda e: e.tensor_copy(out=Wm[:, 8:16, :], in_=v3(pP[1])), ["pP1"], ["Wm"])

            sq_mm(False)
            sq_evac(False)
            for it in range(5):
                w_mm()
                if it < 4:
                    sq_mm(it + 1 == 4)
                w_evac()
                if it < 4:
                    sq_evac(it + 1 == 4)
            if X == _os.environ.get('MPASS', 'A') and int(_os.environ.get('RSTOP', '0')) == 6:
                return
            for c in range(8):
                chv = int(_os.environ.get('CHV', '0'))
                for h in range(2):
                    ph = slice(64 * h, 64 * h + 64)
                    if chv == 6 or (chv == 7 and h == 1) or (chv == 8 and h == 0) or (chv == 9 and (h == 0 or c > 0)):
                        continue
                    P(lambda e, c=c, h=h, ph=ph: e.matmul(pA[0:64, h * 64:(h + 1) * 64], lhsT=RH[ph, c, 0:64], rhs=STb[ph, hp, :], start=True, stop=True), ["RH", "STb"], ["pA"], acc=(h > 0), rg=64 * h)
                for h in range(2):
                    if chv in (5, 7, 8, 9, 10, 11):
                        continue
                    P(lambda e, c=c, h=h: e.matmul(pA[0:64, 128 + h * 64:128 + (h + 1) * 64], lhsT=Nka[:, h * 8 + c, :], rhs=Vtok[:, c, 64 * h:64 * h + 64], start=True, stop=True), [gk(Nka), "Vtok"], ["pA"], acc=True)
                if chv in (5, 6, 7, 8, 9, 10, 11):
                    continue
                A(lambda e: e.copy(out=QTf[:], in_=pA[0:64, 0:128]), ["pA"], ["QTf"])
                V(lambda e: e.tensor_tensor(out=QT[:], in0=QTf[:], in1=pA[0:64, 128:256], op=ALU.add), ["pA", "QTf"], ["QT"])
                if int(_os.environ.get('CHV', '0')) == 3:
                    continue
                for h in range(2):
                    P(lambda e, c=c, h=h: e.matmul(pB[0:64, h * 64:(h + 1) * 64], lhsT=Wm[:, h * 8 + c, :], rhs=QT[:, h * 64:(h + 1) * 64], start=True, stop=True), ["Wm", "QT"], ["pB"], acc=(h > 0))
                V(lambda e: e.tensor_copy(out=PT[:], in_=pB[0:64, 0:128]), ["pB"], ["PT"])
                if int(_os.environ.get('CHV', '0')) == 4:
                    continue
                for h in range(2):
                    if not last_pass and not store:
                        continue
                    ph = slice(64 * h, 64 * h + 64)
                    o1 = pC[0:64, (c % 4) * 128 + h * 64:(c % 4) * 128 + h * 64 + 64]
                    o2 = pD[0:64, (c % 4) * 128 + h * 64:(c % 4) * 128 + h * 64 + 64]
                    P(lambda e, c=c, h=h, ph=ph, o1=o1: e.matmul(o1, lhsT=RH[ph, c, 64:128], rhs=STb[ph, hp, :], start=True, stop=True), ["RH", "STb"], ["pC"], acc=not (c % 4 == 0 and h == 0), rg=64 * h)
                    P(lambda e, c=c, h=h, o2=o2: e.matmul(o2, lhsT=Mbr[:, h * 8 + c, :], rhs=PT[:, h * 64:(h + 1) * 64], start=True, stop=False), [gk(Mbr), "PT"], ["pD"], acc=not (c % 4 == 0 and h == 0))
                    P(lambda e, c=c, h=h, o2=o2: e.matmul(o2, lhsT=Mkr[:, h * 8 + c, :], rhs=Vtok[:, c, 64 * h:64 * h + 64], start=False, stop=True), [gk(Mkr), "Vtok"], ["pD"], acc=True)
                if c % 4 == 3 and (last_pass or store):
                    A(lambda e, c=c: e.copy(out=YT[:, c - 3:c + 1, :], in_=pC[0:64, :].rearrange("p (a b) -> p a b", b=128)), ["pC"], ["YT"])
                    V(lambda e, c=c: e.tensor_tensor(out=YT[:, c - 3:c + 1, :], in0=YT[:, c - 3:c + 1, :], in1=pD[0:64, :].rearrange("p (a b) -> p a b", b=128), op=ALU.add), ["pD", "YT"], ["YT"])
                pS = pP[c % 2]
                if int(_os.environ.get('CHV', '0')) != 2:
                    P(lambda e, c=c, pS=pS: e.matmul(pS[:, 0:128], lhsT=Btok[:, c, :], rhs=PT[:], start=True, stop=False), ["Btok", "PT"], [kk_(pS)])
                    P(lambda e, c=c, pS=pS: e.matmul(pS[:, 0:128], lhsT=Ktok[:, c, :], rhs=Vtok[:, c, :], start=False, stop=True), ["Ktok", "Vtok"], [kk_(pS)], acc=True)
                for h in range(2):
                    if int(_os.environ.get('CHV', '0')) in (1, 2):
                        continue
                    ph = slice(64 * h, 64 * h + 64)
                    V(lambda e, ph=ph, h=h, pS=pS: e.tensor_tensor(out=ST[ph, hp, :], in0=ST[ph, hp, :], in1=pS[ph, 64 * h:64 * h + 64], op=ALU.add), ["ST", kk_(pS)], ["ST"])
                    V(lambda e, ph=ph, c=c: e.tensor_scalar(out=ST[ph, hp, :], in0=ST[ph, hp, :], scalar1=arena[ph, EP_, 64 * c + 63:64 * c + 64], scalar2=None, op0=ALU.mult), ["ST"] + wk(EP_), ["ST"])
                G(lambda e: e.tensor_copy(out=STb[:, hp, :], in_=ST[:, hp, :]), ["ST"], ["STb"])
            if X == _os.environ.get('MPASS', 'A') and int(_os.environ.get('RSTOP', '0')) == 7:
                return
            if not last_pass and not store:
                return
            for c in range(8):
                P(lambda e, c=c: e.transpose(out=pA[:, c * 64:(c + 1) * 64], in_=YT[:, c, :], identity=identf[0:64, 0:64]), ["YT", "c_identf"], ["pA"], acc=(c > 0))
            YR = R_
            A(lambda e: e.copy(out=W(YR), in_=pA[:]), ["pA"], wk(YR))
            if not last_pass:
                if store:
                    LD(lambda e: e.dma_start(out=Y1[j, 1024 + hp * 128:1024 + (hp + 1) * 128, :], in_=W(YR)), wk(YR), ["Y1_%d" % j])
                return
            jm = NT - 1 - j
            LD(lambda e: e.dma_start(out=W(K_), in_=Y1[jm, 1024 + hp * 128:1024 + (hp + 1) * 128, :]), ["Y1_%d" % jm], wk(K_))
            rev = bass.AP(arena[:].tensor, arena[:, K_, 511:512].offset, [list(arena[:].ap[0]), [-1, 512]])
            V(lambda e: e.tensor_tensor(out=W(0), in0=W(YR), in1=rev, op=ALU.add), wk(YR, K_), wk(0))
            P(lambda e: e.matmul(pB[:], lhsT=bones[:], rhs=W(0), start=True, stop=True), ["c_bones"] + wk(0), ["pB"])
            V(lambda e: e.scalar_tensor_tensor(out=W(0), in0=pB[:], scalar=-1.0 / 64, in1=W(0), op0=ALU.mult, op1=ALU.add), ["pB"] + wk(0), wk(0))
            A(lambda e: e.activation(out=W(1), in_=W(0), func=AF.Square), wk(0), wk(1))
            P(lambda e: e.matmul(pC[:], lhsT=bones[:], rhs=W(1), start=True, stop=True), ["c_bones"] + wk(1), ["pC"])
            V(lambda e: e.tensor_scalar(out=W(1), in0=pC[:], scalar1=1.0 / 64, scalar2=64e-5, op0=ALU.mult, op1=ALU.add), ["pC"], wk(1))
            A(lambda e: e.activation(out=W(1), in_=W(1), func=AF.Sqrt), wk(1), wk(1))
            V(lambda e: e.reciprocal(out=W(1), in_=W(1)), wk(1), wk(1))
            V(lambda e: e.tensor_tensor(out=W(0), in0=W(0), in1=W(1), op=ALU.mult), wk(0, 1), wk(0))
            V(lambda e: e.tensor_scalar(out=W(0), in0=W(0), scalar1=col(vec["ln_w"]), scalar2=col(vec["ln_b"]), op0=ALU.mult, op1=ALU.add), wk(0) + ["c_ln_w", "c_ln_b"], wk(0))
            V(lambda e: e.tensor_tensor(out=W(0), in0=W(0), in1=W(BON_), op=ALU.add), wk(0, BON_), wk(0))
            pt, pk = proj(42 + hp, False)
            A(lambda e: e.activation(out=W(1), in_=pt[:], func=AF.Silu), [pk], wk(1))
            V(lambda e: e.tensor_tensor(out=yrwb[:, hp, :], in0=W(0), in1=W(1), op=ALU.mult), wk(0, 1), ["yrwb"])

        glu_v = dr["w_glu"].rearrange("(kc p) m -> p kc m", p=128)
        s5o_v = dr["s5_w_out"].rearrange("(kc p) m -> p kc m", p=128)
        rwo_v = dr["rw_w_out"].rearrange("(kc p) m -> p kc m", p=128)
        wo_v = dr["w_o"].rearrange("(kc p) m -> p kc m", p=128)
        aflat = arena[:].rearrange("p a b -> p (a b)")
        ytok = aflat[:, 0:4096].rearrange("p (a b) -> p a b", b=2048)
        stat2 = sb([128, 8])

        def stage_tail(xsrc, j):
            for cb in range(8):
                wv, wkey = load_wb("glu", cb, 8, 128)
                for kc in range(8):
                    P(lambda e, kc=kc: e.matmul(pA[:], lhsT=wv[:, kc, :], rhs=zb[:, kc, :], start=(kc == 0), stop=(kc == 7)), [wkey, "zb"], ["pA"], acc=(kc > 0))
                A(lambda e: e.activation(out=W(0), in_=pA[:], func=AF.Sigmoid), ["pA"], wk(0))
                pt, pk = proj(8 + cb, False)
                A(lambda e: e.activation(out=W(1), in_=pt[:], func=AF.Silu), [pk], wk(1))
                V(lambda e: e.tensor_tensor(out=W(0), in0=W(0), in1=W(1), op=ALU.mult), wk(0, 1), wk(0))
                V(lambda e, cb=cb: e.tensor_tensor(out=zzb[:, cb, :], in0=zb[:, cb, :], in1=W(0), op=ALU.mult), wk(0) + ["zb"], ["zzb"])
            if int(_os.environ.get('TSTOP', '0')) == 1:
                return
            for ob in range(16):
                wv, wkey = load_wb("s5o", ob, 8, 128)
                for kc in range(8):
                    P(lambda e, kc=kc: e.matmul(pA[:], lhsT=wv[:, kc, :], rhs=zzb[:, kc, :], start=(kc == 0), stop=(kc == 7)), [wkey, "zzb"], ["pA"], acc=(kc > 0))
                wv2, wkey2 = load_wb("rwo", ob, 8, 128)
                for kc in range(8):
                    P(lambda e, kc=kc: e.matmul(pB[:], lhsT=wv2[:, kc, :], rhs=yrwb[:, kc, :], start=(kc == 0), stop=(kc == 7)), [wkey2, "yrwb"], ["pB"], acc=(kc > 0))
                pt0, pk0 = proj(50 + ob, False)
                A(lambda e: e.activation(out=W(8), in_=pt0[:], func=AF.Sigmoid), [pk0], wk(8))
                pt1, pk1 = proj(66 + ob, False)
                A(lambda e: e.activation(out=W(9), in_=pt1[:], func=AF.Sigmoid), [pk1], wk(9))
                V(lambda e: e.tensor_tensor(out=W(8), in0=W(8), in1=pA[:], op=ALU.mult), wk(8) + ["pA"], wk(8))
                V(lambda e: e.tensor_tensor(out=W(9), in0=W(9), in1=pB[:], op=ALU.mult), wk(9) + ["pB"], wk(9))
                G(lambda e, ob=ob: e.tensor_tensor(out=hb[:, ob, :], in0=W(8), in1=W(9), op=ALU.add), wk(8, 9), ["hb"])
            if int(_os.environ.get('TSTOP', '0')) == 2:
                return
            yk = wk(*range(8))
            for tqp in range(2):
                V(lambda e: e.memset(stat2[:], 0.0), [], ["stat2"])
                for nb in range(4):
                    LD(lambda e, nb=nb: e.dma_start(out=gsl, in_=dr["post_g"][:, nb * 512:(nb + 1) * 512]), [], ["s5t"])
                    for kq in range(4):
                        wv, wkey = load_wb("wo", nb * 4 + kq, 4, 512)
                        for t2 in range(2):
                            tq = tqp * 2 + t2
                            pX = (pC, pD)[t2]
                            for k4 in range(4):
                                P(lambda e, k4=k4, kq=kq, tq=tq, pX=pX: e.matmul(pX[:], lhsT=hb[:, kq * 4 + k4, tq * 128:(tq + 1) * 128], rhs=wv[:, k4, :],
                                                                          start=(kq == 0 and k4 == 0), stop=(kq == 3 and k4 == 3)),
                                  [wkey, "hb"], [kk_(pX)], acc=not (kq == 0 and k4 == 0))
                    if int(_os.environ.get('TSTOP', '0')) == 3:
                        return
                    for t2 in range(2):
                        pX = (pC, pD)[t2]
                        if int(_os.environ.get('TSKIP', '0')) != 1:
                            A(lambda e, t2=t2, nb=nb, pX=pX: e.activation(out=xn[:, 0:512], in_=pX[:], func=AF.Square, accum_out=stat2[:, t2 * 4 + nb:t2 * 4 + nb + 1]),
                              [kk_(pX), "stat2"], ["xn", "stat2"])
                        if int(_os.environ.get('TSKIP', '0')) != 2:
                            V(lambda e, t2=t2, nb=nb, pX=pX: e.tensor_tensor(out=ytok[:, t2, nb * 512:(nb + 1) * 512], in0=pX[:], in1=gsl, op=ALU.mult),
                              [kk_(pX), "s5t", "xn"], yk)
                if int(_os.environ.get('TSTOP', '0')) == 4:
                    return
                for t2 in range(2):
                    tq = tqp * 2 + t2
                    V(lambda e, t2=t2: e.tensor_reduce(out=stat[:, 0:1], in_=stat2[:, t2 * 4:(t2 + 1) * 4], axis=mybir.AxisListType.X, op=ALU.add), ["stat2"], ["stat"])
                    V(lambda e: e.tensor_scalar(out=stat[:, 1:2], in0=stat[:, 0:1], scalar1=1.0 / D, scalar2=1e-6, op0=ALU.mult, op1=ALU.add), ["stat"], ["stat"])
                    A(lambda e: e.activation(out=stat[:, 1:2], in_=stat[:, 1:2], func=AF.Sqrt), ["stat"], ["stat"])
                    V(lambda e: e.reciprocal(out=stat[:, 2:3], in_=stat[:, 1:2]), ["stat"], ["stat"])
                    LD(lambda e, tq=tq: e.dma_start(out=xs[:, :], in_=xsrc[j, 1 + 128 * tq:129 + 128 * tq, :]), [], ["xs"])
                    V(lambda e, t2=t2: e.scalar_tensor_tensor(out=ytok[:, t2, :], in0=ytok[:, t2, :], scalar=stat[:, 2:3], in1=xs[:, :], op0=ALU.mult, op1=ALU.add),
                      yk + ["stat", "xs"], yk)
                    LD(lambda e, t2=t2, tq=tq: e.dma_start(out=yB[j * TT + tq * 128:j * TT + (tq + 1) * 128, :], in_=ytok[:, t2, :]), yk, ["yB"])

        stf = ST[:].rearrange("p a b -> p (a b)")
        xsel = sb([128, 8])
        LD(lambda e: e.dma_start(out=xsel[:], in_=dr["xsel"]), [], ["c_xsel"])
        XW = 8 * 64 + 128
        xst_out = nc.dram_tensor("xst_out", [128, XW], F32, kind="Internal").ap()

        def save_state():
            LD(lambda e: e.dma_start(out=xst_out[:, 0:512], in_=stf), ["ST"], ["xst_out"])
            LD(lambda e: e.dma_start(out=xst_out[:, 512:576], in_=X1c[:]), ["X1c"], ["xst_out"])
            LD(lambda e: e.dma_start(out=xst_out[:, 576:640], in_=X2c[:]), ["X2c"], ["xst_out"])

        def restore_state():
            LD(lambda e: e.dma_start(out=stf, in_=xst_out[:, 0:512]), ["xst_out"], ["ST"])
            LD(lambda e: e.dma_start(out=X1c[:], in_=xst_out[:, 512:576]), ["xst_out"], ["X1c"])
            LD(lambda e: e.dma_start(out=X2c[:], in_=xst_out[:, 576:640]), ["xst_out"], ["X2c"])

        for cb in range(82):
            precast("in", cb, w_in_v[:, :, cb * 128:(cb + 1) * 128], 16, 128)
        for cb in range(8):
            precast("glu", cb, glu_v[:, :, cb * 128:(cb + 1) * 128], 8, 128)
        for ob in range(16):
            precast("s5o", ob, s5o_v[:, :, ob * 128:(ob + 1) * 128], 8, 128)
            precast("rwo", ob, rwo_v[:, :, ob * 128:(ob + 1) * 128], 8, 128)
        for nb in range(4):
            for kq in range(4):
                precast("wo", nb * 4 + kq, wo_v[:, kq * 4:(kq + 1) * 4, nb * 512:(nb + 1) * 512], 4, 512)

        plist = [("P", "B", "xP", False, False), ("A", "A", "xA", False, True), ("B", "B", "xB", True, True)]
        plist = [p for p in plist if p[0] in passes]
        for pi3, (pname, X, xname, last, store) in enumerate(plist):
            pi = ("P", "A", "B").index(pname)
            xsrc = dr[xname]
            if pname == "A" and "P" in passes:
                save_state()
            if pname == "B" and "P" in passes:
                restore_state()
            for j in range(NT):
                cc = cm[:, pi * NT + j:pi * NT + j + 1]
                V(lambda e, cc=cc: e.tensor_scalar(out=stf, in0=stf, scalar1=cc, scalar2=None, op0=ALU.mult), ["ST", "c_cm"], ["ST"])
                V(lambda e: e.tensor_copy(out=STb[:].rearrange("p a b -> p (a b)"), in_=stf), ["ST"], ["STb"])
                V(lambda e, cc=cc: e.tensor_scalar(out=X1c[:], in0=X1c[:], scalar1=cc, scalar2=None, op0=ALU.mult), ["X1c", "c_cm"], ["X1c"])
                V(lambda e, cc=cc: e.tensor_scalar(out=X2c[:], in0=X2c[:], scalar1=cc, scalar2=None, op0=ALU.mult), ["X2c", "c_cm"], ["X2c"])
                ms = int(_os.environ.get('MSTOP', '0')) if X == _os.environ.get('MPASS', 'A') else 0
                if ms == 1:
                    break
                stage_load(xsrc, j)
                if ms == 2:
                    break
                for cb in range(8):
                    stage_s5(X, j, cb, last, store)
                    if ms == 3:
                        break
                if ms in (3, 4):
                    break
                stage_lora(X)
                if ms == 5:
                    break
                for hp in range(8):
                    stage_rwkv(X, j, hp, last, store)
                    if ms == 6:
                        break
                if ms in (6, 7):
                    break
                if last:
                    stage_tail(xsrc, j)
            if int(_os.environ.get('MSTOP', '0')) > 0 and X == _os.environ.get('MPASS', 'A'):
                break
        S.finish("sp")
    S.close()
    return nc, S


def kernel(**inputs):
    runs, NT = make_plan("bal")
    partners = [1, 0, 3, 2, None, None, None, None]
    in_maps, metas, ntok = host_layout(inputs, runs, NT, partners)
    nc, _ = build_program(NT, n_cores=8)
    res = run_bass_kernel_spmd(nc, in_maps, core_ids=list(range(8)))
    y_all = np.zeros((ntok, D), np.float32)
    for c, r in enumerate(res.results):
        yb = np.asarray(r["yB"])
        gi = metas[c]["gidxB"]
        valid = gi < ntok
        y_all[gi[valid]] = yb[valid]
    return (y_all[:32768].reshape(2, 16384, D), y_all[32768:].reshape(16, 2048, D))
```
